# Optimizing a Trainium2 kernel written in Bass

```python
import math
import jax, jax.numpy as jnp
from jax import lax
import numpy as np

D_MODEL = 1024
BATCH = 16
SEQ = 256
DEPTH = 4
DEC_BATCH = 8
DEC_SEQ = 1024
PAST_LEN = 256

GRID_W = 64
N_MIXERS = 3
N_ATTN_LAYERS = (DEPTH + 2) // 3
N_RWKV_LAYERS = (DEPTH + 1) // 3
N_CONV_LAYERS = DEPTH // 3
D_INNER = D_MODEL
NORM_EPS = 1e-6
ATTN_HEAD_DIM = 64
ATTN_HEADS = D_INNER // (2 * ATTN_HEAD_DIM)
ATTN_V_DIM = 2 * ATTN_HEAD_DIM
ROPE_BASE = 10000.0
Q_BLOCK = 128
RWKV_HEAD_DIM = 64
RWKV_HEADS = D_INNER // RWKV_HEAD_DIM
DECAY_LORA = 64
ICLR_LORA = 64
RWKV_GN_EPS = RWKV_HEAD_DIM * 1e-5
N_SHIFT_MIX = 6
CONV_WIDTH = 31

kernel_name = 'hybrid_diffattn_rwkv7_conformer_dit_step'

F32 = jnp.float32


def rms_norm(x, w, eps=NORM_EPS):
    xf = x.astype(F32)
    y = xf * lax.rsqrt(jnp.mean(xf * xf, axis=-1, keepdims=True) + eps)
    return (y * w.astype(F32)).astype(x.dtype)


def layer_norm(x, w, b, eps=1e-5):
    xf = x.astype(F32)
    m = jnp.mean(xf, axis=-1, keepdims=True)
    var = jnp.mean(jnp.square(xf - m), axis=-1, keepdims=True)
    y = (xf - m) * lax.rsqrt(var + eps)
    return (y * w.astype(F32) + b.astype(F32)).astype(x.dtype)


def adaln(cond, w, b):
    m = jnp.einsum('...d,de->...e', jax.nn.silu(cond), w) + b
    return jnp.split(m, 3, axis=-1)


def axial_rope_tables(n_tok):
    rows = n_tok // GRID_W
    rr, cc = jnp.meshgrid(jnp.arange(rows, dtype=F32), jnp.arange(GRID_W, dtype=F32), indexing='ij')
    row, col = rr.reshape(-1), cc.reshape(-1)
    axis_dim = ATTN_HEAD_DIM // 2
    inv_freq = ROPE_BASE ** (-jnp.arange(0, axis_dim, 2, dtype=F32) / axis_dim)
    ang_r = row[:, None] * inv_freq
    ang_c = col[:, None] * inv_freq
    shp = (n_tok, 1, 1, axis_dim // 2)
    return (jnp.cos(ang_r).reshape(shp), jnp.sin(ang_r).reshape(shp),
            jnp.cos(ang_c).reshape(shp), jnp.sin(ang_c).reshape(shp))


def _rotate(x, cos, sin):
    x1, x2 = jnp.split(x, 2, axis=-1)
    return jnp.concatenate([x1 * cos - x2 * sin, x1 * sin + x2 * cos], axis=-1)


def axial_rope(x, tables):
    cr, sr, cc, sc = tables
    xr, xc = jnp.split(x, 2, axis=-1)
    return jnp.concatenate([_rotate(xr, cr, sr), _rotate(xc, cc, sc)], axis=-1).astype(x.dtype)


def diff_softmax_attend(q, k, v, lam):
    B, H, Tq, _ = q.shape
    qb = Q_BLOCK if Tq % Q_BLOCK == 0 else Tq
    nb = Tq // qb
    k1, k2 = jnp.split(k, 2, axis=-1)
    scale = ATTN_HEAD_DIM ** -0.5

    def block(q_blk):
        q1, q2 = jnp.split(q_blk, 2, axis=-1)
        s1 = jnp.einsum('bhqd,bhkd->bhqk', q1, k1).astype(F32) * scale
        s2 = jnp.einsum('bhqd,bhkd->bhqk', q2, k2).astype(F32) * scale
        p = jax.nn.softmax(s1, axis=-1) - lam * jax.nn.softmax(s2, axis=-1)
        return jnp.einsum('bhqk,bhkd->bhqd', p.astype(v.dtype), v)

    q_blocks = jnp.moveaxis(q.reshape(B, H, nb, qb, q.shape[-1]), 2, 0)
    out = lax.map(block, q_blocks)
    return jnp.moveaxis(out, 0, 2).reshape(B, H, Tq, v.shape[-1])


def diff_attn_mixer(h, layer_idx, w_in, lam_vecs, subln_w, w_out, rope=None, ctx_k=None, ctx_v=None):
    B, T, _ = h.shape
    q, k, v, g = jnp.split(h @ w_in, 4, axis=-1)
    q = q.reshape(B, T, ATTN_HEADS, 2, ATTN_HEAD_DIM)
    k = k.reshape(B, T, ATTN_HEADS, 2, ATTN_HEAD_DIM)
    if rope is not None:
        q = axial_rope(q, rope)
        k = axial_rope(k, rope)
    heads = lambda t: t.reshape(B, T, ATTN_HEADS, -1).transpose(0, 2, 1, 3)
    q, k, v = heads(q), heads(k), heads(v)
    own_k, own_v = k, v
    if ctx_k is not None:
        k = jnp.concatenate([k, ctx_k.astype(k.dtype)], axis=2)
        v = jnp.concatenate([v, ctx_v.astype(v.dtype)], axis=2)
    lam_init = 0.8 - 0.6 * math.exp(-0.3 * layer_idx)
    lf = lam_vecs.astype(F32)
    lam = jnp.exp(jnp.sum(lf[0] * lf[1])) - jnp.exp(jnp.sum(lf[2] * lf[3])) + lam_init
    o = diff_softmax_attend(q, k, v, lam)
    o = rms_norm(o, subln_w, eps=1e-5) * (1.0 - lam_init)
    o = o.transpose(0, 2, 1, 3).reshape(B, T, D_INNER)
    return (o * jax.nn.silu(g)) @ w_out, own_k, own_v


def rwkv7_mixer(h, mu, w_in, w0, w1, w2, a0, a1, a2, k_k, k_a, r_k, ln_w, ln_b, w_out, init_state=None):
    B, T, D = h.shape
    H, N = RWKV_HEADS, RWKV_HEAD_DIM
    hp = jnp.pad(h, ((0, 0), (1, 1), (0, 0)))
    dx = 0.5 * (hp[:, :-2] + hp[:, 2:]) - h
    xs = h[None] + dx[None] * mu[:, None, None, :]
    r, k, v, g = jnp.einsum('nbtd,dne->nbte', xs[:4], w_in.reshape(D, 4, D_INNER))
    lw = jnp.einsum('zbtr,zre->zbte', jnp.tanh(jnp.einsum('btd,zdr->zbtr', xs[4], w1)), w2)
    w_log = -jax.nn.softplus(-(w0[:, None, None, :] + lw).astype(F32)) - 0.5
    decay = jnp.exp(-jnp.exp(w_log))
    la = jnp.einsum('zbtr,zre->zbte', jnp.einsum('btd,zdr->zbtr', xs[5], a1), a2)
    a = jax.nn.sigmoid((a0[:, None, None, :] + la).astype(F32))
    r32, k32, v32 = r.astype(F32), k.astype(F32), v.astype(F32)
    kk = (k32 * k_k.astype(F32)).reshape(B, T, H, N)
    kk = kk * lax.rsqrt(jnp.maximum(jnp.sum(kk * kk, axis=-1, keepdims=True), 1e-24))
    kk = kk.reshape(B, T, D_INNER)
    k_dir = k32[None] * (1.0 + (a - 1.0) * k_a.astype(F32))

    both = lambda x2: jnp.stack([x2[0], jnp.flip(x2[1], axis=1)])
    shared = lambda x: jnp.stack([x, jnp.flip(x, axis=1)])
    to_scan = lambda x2: jnp.moveaxis(x2.reshape(2, B, T, H, N), 2, 0)
    seq = (to_scan(shared(r32)), to_scan(both(decay)), to_scan(both(k_dir)),
           to_scan(shared(v32)), to_scan(shared(kk)), to_scan(both(a)))

    def step(S, inp):
        r_t, w_t, k_t, v_t, kk_t, a_t = inp
        sa = jnp.einsum('zbhij,zbhj->zbhi', S, -kk_t)
        S = S * w_t[..., None, :] + sa[..., None] * (kk_t * a_t)[..., None, :] + v_t[..., None] * k_t[..., None, :]
        return S, jnp.einsum('zbhij,zbhj->zbhi', S, r_t)

    if init_state is None:
        S0 = jnp.zeros((2, B, H, N, N), F32)
    else:
        S0 = jnp.moveaxis(init_state.astype(F32), 1, 0)
    S_final, ys = lax.scan(step, S0, seq)
    ys = jnp.moveaxis(ys, 0, 2)
    y = ys[0] + jnp.flip(ys[1], axis=1)
    m = jnp.mean(y, axis=-1, keepdims=True)
    var = jnp.mean(jnp.square(y - m), axis=-1, keepdims=True)
    y = ((y - m) * lax.rsqrt(var + RWKV_GN_EPS)).reshape(B, T, D_INNER)
    y = y * ln_w.astype(F32) + ln_b.astype(F32)
    rk = jnp.sum((r32[None] * k_dir * r_k.reshape(-1).astype(F32)).reshape(2, B, T, H, N), axis=-1, keepdims=True)
    bonus = jnp.sum(rk * v32.reshape(B, T, H, N)[None], axis=0).reshape(B, T, D_INNER)
    y = (y + bonus).astype(h.dtype)
    return (y * jax.nn.silu(g)) @ w_out, jnp.moveaxis(S_final, 0, 1)


def conformer_conv_mixer(h, w_in, dw_w, dw_b, ln_w, ln_b, w_out):
    a, b, g = jnp.split(h @ w_in, 3, axis=-1)
    z = a * jax.nn.sigmoid(b)
    pad = CONV_WIDTH // 2
    z = lax.conv_general_dilated(z, dw_w[:, None, :].astype(z.dtype), window_strides=(1,),
                                 padding=[(pad, pad)], dimension_numbers=('NWC', 'WIO', 'NWC'),
                                 feature_group_count=D_INNER) + dw_b
    z = jax.nn.silu(layer_norm(z, ln_w, ln_b))
    return (z * jax.nn.silu(g)) @ w_out


def setup_inputs(seed: int = 0) -> dict:
    key = jax.random.key(seed)
    ks = iter(jax.random.split(key, 40))
    nrm = lambda shape, s: jax.random.normal(next(ks), shape, F32) * s
    D, E = D_MODEL, D_INNER
    NA, NB, NC = N_ATTN_LAYERS, N_RWKV_LAYERS, N_CONV_LAYERS
    return {
        'x_prompt': nrm((BATCH, SEQ, D), 1.0),
        'x_sample': nrm((DEC_BATCH, DEC_SEQ, D), 1.0),
        'cache_attn_k': nrm((DEC_BATCH, NA, ATTN_HEADS, PAST_LEN, 2 * ATTN_HEAD_DIM), 1.0),
        'cache_attn_v': nrm((DEC_BATCH, NA, ATTN_HEADS, PAST_LEN, ATTN_V_DIM), 1.0),
        'state_rwkv': nrm((DEC_BATCH, NB, 2, RWKV_HEADS, RWKV_HEAD_DIM, RWKV_HEAD_DIM), 0.3),
        'c': nrm((DEC_BATCH, D), 1.0),
        'c_ctx': nrm((D,), 1.0),
        'norm_w': 1.0 + nrm((DEPTH, D), 0.02),
        'ada_w': nrm((DEPTH, D, 3 * D), 0.5 * D ** -0.5),
        'ada_b': nrm((DEPTH, 3 * D), 0.01),
        'attn_w_in': nrm((NA, D, 4 * E), D ** -0.5),
        'attn_lambda': nrm((NA, 4, ATTN_HEAD_DIM), 0.1),
        'attn_subln_w': 1.0 + nrm((NA, ATTN_V_DIM), 0.02),
        'attn_w_out': nrm((NA, E, D), E ** -0.5),
        'rwkv_mu': jax.random.uniform(next(ks), (NB, N_SHIFT_MIX, D), F32),
        'rwkv_w_in': nrm((NB, D, 4 * E), D ** -0.5),
        'rwkv_w0': nrm((NB, 2, E), 0.5),
        'rwkv_w1': nrm((NB, 2, D, DECAY_LORA), D ** -0.5),
        'rwkv_w2': nrm((NB, 2, DECAY_LORA, E), 0.1 * DECAY_LORA ** -0.5),
        'rwkv_a0': nrm((NB, 2, E), 0.1),
        'rwkv_a1': nrm((NB, 2, D, ICLR_LORA), D ** -0.5),
        'rwkv_a2': nrm((NB, 2, ICLR_LORA, E), 0.5 * ICLR_LORA ** -0.5),
        'rwkv_k_k': 0.85 + nrm((NB, E), 0.05),
        'rwkv_k_a': 1.0 + nrm((NB, E), 0.05),
        'rwkv_r_k': nrm((NB, RWKV_HEADS, RWKV_HEAD_DIM), 0.1),
        'rwkv_ln_w': 1.0 + nrm((NB, E), 0.02),
        'rwkv_ln_b': nrm((NB, E), 0.01),
        'rwkv_w_out': nrm((NB, E, D), E ** -0.5),
        'conv_w_in': nrm((NC, D, 3 * E), D ** -0.5),
        'conv_dw_w': nrm((NC, CONV_WIDTH, E), CONV_WIDTH ** -0.5),
        'conv_dw_b': nrm((NC, E), 0.01),
        'conv_ln_w': 1.0 + nrm((NC, E), 0.02),
        'conv_ln_b': nrm((NC, E), 0.01),
        'conv_w_out': nrm((NC, E, D), E ** -0.5),
        'final_norm_w': 1.0 + nrm((D,), 0.02),
    }


def reference(x_prompt, x_sample, cache_attn_k, cache_attn_v, state_rwkv, c, c_ctx,
              norm_w, ada_w, ada_b, attn_w_in, attn_lambda, attn_subln_w, attn_w_out,
              rwkv_mu, rwkv_w_in, rwkv_w0, rwkv_w1, rwkv_w2, rwkv_a0, rwkv_a1, rwkv_a2,
              rwkv_k_k, rwkv_k_a, rwkv_r_k, rwkv_ln_w, rwkv_ln_b, rwkv_w_out,
              conv_w_in, conv_dw_w, conv_dw_b, conv_ln_w, conv_ln_b, conv_w_out, final_norm_w):
    rope = axial_rope_tables(x_sample.shape[1])
    cond_ctx = c_ctx[None, None, :]
    cond_lat = c[:, None, :]
    xp, xs = x_prompt, x_sample
    new_k, new_v, new_s = [], [], []
    for i in range(DEPTH):
        kind, j = i % N_MIXERS, i // N_MIXERS
        sh_p, sc_p, gt_p = adaln(cond_ctx, ada_w[i], ada_b[i])
        sh_s, sc_s, gt_s = adaln(cond_lat, ada_w[i], ada_b[i])
        hp = rms_norm(xp, norm_w[i]) * (1 + sc_p) + sh_p
        hs = rms_norm(xs, norm_w[i]) * (1 + sc_s) + sh_s
        if kind == 0:
            op, kc, vc = diff_attn_mixer(hp, i, attn_w_in[j], attn_lambda[j], attn_subln_w[j], attn_w_out[j])
            os_, _, _ = diff_attn_mixer(hs, i, attn_w_in[j], attn_lambda[j], attn_subln_w[j], attn_w_out[j],
                                        rope=rope, ctx_k=cache_attn_k[:, j], ctx_v=cache_attn_v[:, j])
            new_k.append(kc)
            new_v.append(vc)
        elif kind == 1:
            rw = (rwkv_mu[j], rwkv_w_in[j], rwkv_w0[j], rwkv_w1[j], rwkv_w2[j], rwkv_a0[j], rwkv_a1[j],
                  rwkv_a2[j], rwkv_k_k[j], rwkv_k_a[j], rwkv_r_k[j], rwkv_ln_w[j], rwkv_ln_b[j], rwkv_w_out[j])
            op, st = rwkv7_mixer(hp, *rw)
            os_, _ = rwkv7_mixer(hs, *rw, init_state=state_rwkv[:, j])
            new_s.append(st)
        else:
            cw = (conv_w_in[j], conv_dw_w[j], conv_dw_b[j], conv_ln_w[j], conv_ln_b[j], conv_w_out[j])
            op = conformer_conv_mixer(hp, *cw)
            os_ = conformer_conv_mixer(hs, *cw)
        xp = xp + gt_p * op
        xs = xs + gt_s * os_
    y_prompt = rms_norm(xp, final_norm_w)
    y_sample = rms_norm(xs, final_norm_w)
    new_attn_k = jnp.stack(new_k, axis=1)
    new_attn_v = jnp.stack(new_v, axis=1)
    new_rwkv_state = jnp.stack(new_s, axis=1)
    return (y_prompt, y_sample, new_attn_k, new_attn_v, new_rwkv_state)
```

```python
from contextlib import ExitStack
import math
import numpy as np
import concourse.bass as bass
import concourse.mybir as mybir
from concourse.bass_utils import run_bass_kernel_spmd

F32 = mybir.dt.float32
BF16 = mybir.dt.bfloat16
AF = mybir.ActivationFunctionType
ALU = mybir.AluOpType
AX = mybir.AxisListType

ENGS = ("pe", "act", "dve", "pool", "sp")
SB_BASE = 16512
SB_END = 229376
SEM_CAP = 8000
SAME_ENG_SAFE_DIST = 10 ** 9
RELAX_SAME_ENG = False


class Buf:
    def __init__(self, t, name, space):
        self.t = t
        self.name = name
        self.space = space
        self.last_w = None
        self.readers = {}
        self.dma_readers = []
        self.dma_sem = None
        self.dma_count = 0

    def __getitem__(self, k):
        return self.t[k]


class Op:
    __slots__ = ("eng", "fn", "deps", "is_dma", "idx", "signal", "sig_val", "sem_owner", "dma_val", "tag")

    def __init__(self, eng, fn, is_dma=False, tag=""):
        self.eng = eng
        self.fn = fn
        self.deps = []
        self.is_dma = is_dma
        self.idx = -1
        self.signal = False
        self.sig_val = 0
        self.sem_owner = None
        self.dma_val = 0
        self.tag = tag


class Ctx:
    def __init__(self, nc):
        self.nc = nc
        self.stack = ExitStack()
        self.bufs = {}
        self.ops = {e: [] for e in ENGS}
        self.out_dma_ops = []
        self.n_waits = 0
        self.n_sems = 0
        self.sb_ptr = SB_BASE
        self.nview = 0

    def sbuf(self, name, shape, dtype=F32):
        esz = 2 if dtype == BF16 else 4
        nbytes = int(np.prod(shape[1:])) * esz
        nbytes = (nbytes + 31) // 32 * 32
        off = self.sb_ptr
        self.sb_ptr += nbytes
        assert self.sb_ptr <= SB_END, f"SBUF overflow at {name}: {self.sb_ptr}"
        t = self.nc.alloc_sbuf_tensor_at(name, list(shape), dtype, offset=off)
        b = Buf(t, name, "sbuf")
        b.off = off
        b.nbytes = nbytes
        self.bufs[t.name] = b
        return b

    def view(self, buf, name, shape, dtype=F32, byte_off=0):
        esz = 2 if dtype == BF16 else 4
        nbytes = int(np.prod(shape[1:])) * esz
        assert byte_off + nbytes <= buf.nbytes, (name, byte_off, nbytes, buf.nbytes)
        self.nview += 1
        t = self.nc.alloc_sbuf_tensor_at(f"{name}_v{self.nview}", list(shape), dtype, offset=buf.off + byte_off)
        self.bufs[t.name] = buf
        return t

    def psum(self, name, shape, dtype=F32):
        t = self.stack.enter_context(self.nc.psum_tensor(name, list(shape), dtype))
        b = Buf(t, name, "psum")
        self.bufs[name] = b
        return b

    def dram_in(self, name, shape, dtype=F32):
        return self.nc.dram_tensor(name, list(shape), dtype, kind="ExternalInput")

    def dram_out(self, name, shape, dtype=F32):
        return self.nc.dram_tensor(name, list(shape), dtype, kind="ExternalOutput")

    def dram_tmp(self, name, shape, dtype=F32):
        t = self.nc.dram_tensor(name, list(shape), dtype, kind="Internal")
        b = Buf(t, name, "dram")
        self.bufs[name] = b
        return b

    def _buf_of(self, ap):
        t = getattr(ap, "tensor", None)
        if t is None:
            return None
        return self.bufs.get(t.name)

    def _add(self, op, reads, writes):
        eng = op.eng
        lst = self.ops[eng]
        op.idx = len(lst)
        deps = []
        raw = set()
        for b in reads:
            if b.last_w is not None:
                deps.append(b.last_w)
                raw.add(id(b.last_w))
            if b.space == "psum":
                for e2, r in b.readers.items():
                    if e2 != eng:
                        deps.append(r)
        for b in writes:
            if b.last_w is not None:
                deps.append(b.last_w)
            deps.extend(b.readers.values())
            deps.extend(b.dma_readers)
        seen = set()
        for d in deps:
            if d is op or id(d) in seen:
                continue
            seen.add(id(d))
            if (not d.is_dma) and d.eng == eng and not op.is_dma:
                if eng == "pe":
                    continue
                if op.idx - d.idx >= SAME_ENG_SAFE_DIST:
                    continue
                if RELAX_SAME_ENG and id(d) not in raw:
                    continue
            op.deps.append(d)
        for b in reads:
            if op.is_dma:
                b.dma_readers.append(op)
            else:
                b.readers[eng] = op
        for b in writes:
            b.last_w = op
            b.readers = {}
            b.dma_readers = []
        lst.append(op)
        return op

    def op(self, eng, method, *args, **kwargs):
        rb, wb = list(kwargs.pop("_xr", [])), []
        items = list(enumerate(args)) + list(kwargs.items())
        for k, v in items:
            if not hasattr(v, "tensor"):
                continue
            b = self._buf_of(v)
            if b is None:
                continue
            if k in ("out", "accum_out") or k == 0:
                wb.append(b)
            else:
                rb.append(b)

        def fn(e, method=method, args=args, kwargs=kwargs):
            return getattr(e, method)(*args, **kwargs)

        return self._add(Op(eng, fn, tag=method), rb, wb)

    def dma(self, eng, out, in_, **kwargs):
        ob = self._buf_of(out)
        ib = self._buf_of(in_)
        if ob is not None and ob.space != "dram":
            owner = ob
        elif ib is not None and ib.space != "dram":
            owner = ib
        else:
            owner = ob if ob is not None else ib
        assert owner is not None

        def fn(e, out=out, in_=in_, kwargs=kwargs):
            return e.dma_start(out=out, in_=in_, **kwargs)

        o = Op(eng, fn, is_dma=True, tag="dma")
        o.sem_owner = owner
        owner.dma_count += 16
        o.dma_val = owner.dma_count
        self._add(o, [ib] if ib is not None else [], [ob] if ob is not None else [])
        if ob is None:
            self.out_dma_ops.append(o)
        return o

    def emit(self):
        nc = self.nc
        for e in ENGS:
            for o in self.ops[e]:
                for d in o.deps:
                    if not d.is_dma:
                        d.signal = True
        nsem = {}
        for e in ENGS:
            cnt = 0
            for o in self.ops[e]:
                if (not o.is_dma) and o.signal:
                    cnt += 1
                    o.sig_val = cnt
            nsem[e] = (cnt + SEM_CAP - 1) // SEM_CAP
        sems = {}
        for e in ENGS:
            sems[e] = [self.stack.enter_context(nc.semaphore(f"s_{e}_{i}")) for i in range(nsem[e])]
        ub = {id(b): b for b in self.bufs.values()}
        for b in ub.values():
            if b.dma_count > 0:
                b.dma_sem = self.stack.enter_context(nc.semaphore(f"d_{b.name}"))
        self.n_sems = sum(len(v) for v in sems.values()) + sum(1 for b in ub.values() if b.dma_sem is not None)

        def emit_engine(eng_name, e):
            waited = {}
            maxsem = {}
            for o in self.ops[eng_name]:
                need = {}
                for d in o.deps:
                    if d.is_dma:
                        key = ("dma", d.sem_owner.name)
                        val = d.dma_val
                        sem = d.sem_owner.dma_sem
                    else:
                        si = (d.sig_val - 1) // SEM_CAP
                        if maxsem.get(d.eng, -1) > si:
                            continue
                        key = (d.eng, si)
                        val = (d.sig_val - 1) % SEM_CAP + 1
                        sem = sems[d.eng][si]
                    if waited.get(key, 0) >= val:
                        continue
                    if key not in need or need[key][1] < val:
                        need[key] = (sem, val)
                if DBG.get("dump"):
                    DUMP.append((eng_name, o.idx, o.tag, o.sig_val if o.signal else 0, o.dma_val if o.is_dma else 0,
                                 o.sem_owner.name if o.is_dma else "", sorted((str(k), v[1]) for k, v in need.items())))
                for key, (sem, val) in need.items():
                    e.wait_ge(sem, val)
                    waited[key] = val
                    if key[0] != "dma":
                        maxsem[key[0]] = max(maxsem.get(key[0], -1), key[1])
                    self.n_waits += 1
                ins = o.fn(e)
                if o.is_dma:
                    ins.then_inc(o.sem_owner.dma_sem, 16)
                elif o.signal:
                    si = (o.sig_val - 1) // SEM_CAP
                    ins.then_inc(sems[eng_name][si], 1)
            if eng_name == "sp":
                fin = {}
                for o in self.out_dma_ops:
                    key = o.sem_owner.name
                    if key not in fin or fin[key][1] < o.dma_val:
                        fin[key] = (o.sem_owner.dma_sem, o.dma_val)
                for key, (sem, val) in fin.items():
                    e.wait_ge(sem, val)

        with nc.Block() as block:
            @block.tensor
            def _(e):
                emit_engine("pe", e)

            @block.scalar
            def _(e):
                emit_engine("act", e)

            @block.vector
            def _(e):
                emit_engine("dve", e)

            @block.gpsimd
            def _(e):
                emit_engine("pool", e)

            @block.sync
            def _(e):
                emit_engine("sp", e)


class Rot:
    def __init__(self, bufs):
        self.bufs = bufs
        self.i = 0

    def next(self):
        b = self.bufs[self.i % len(self.bufs)]
        self.i += 1
        return b


D = 1024
KC = 8
NT = 1536
DEPTH = 4
SEGS = [(0, 1024), (1024, 256), (1280, 256)]
TCH = [(0, 512, 0), (512, 512, 0), (1024, 512, 1)]
NORM_EPS = 1e-6
CONVW = 31
PADW = 15
POFF = [0, 1024 + 30, 1024 + 30 + 256 + 30]
PTOT = 1024 + 256 + 256 + 90


def vec_layout():
    off = {}
    n = 0

    def add(name, cols):
        nonlocal n
        off[name] = n
        n += cols

    for i in range(DEPTH):
        add(f"norm_w{i}", 8)
        add(f"ada_b{i}", 24)
    add("final_w", 8)
    add("subln0", 1)
    add("subln1", 1)
    for k in range(6):
        add(f"mu{k}", 8)
    for z in range(2):
        add(f"w0_{z}", 8)
        add(f"a0_{z}", 8)
    for nm in ("k_k", "k_a", "r_k", "rln_w", "rln_b", "cdw_b", "cln_w", "cln_b"):
        add(nm, 8)
    add("cdw_w", 8 * CONVW)
    return off, n


VOFF, NVEC = vec_layout()


def fm(v):
    return np.ascontiguousarray(np.asarray(v, np.float32).reshape(8, 128).T)


DBG = {}
DUMP = []


def build_program(n_layers=4, debug_x=False):
    nc = bass.Bass("TRN2", target_bir_lowering=False)
    c = Ctx(nc)

    d_xT = c.dram_in("xT_in", [D, NT]).ap()
    d_cond = c.dram_in("condT", [128, 16]).ap()
    d_vecs = c.dram_in("vecs", [128, NVEC]).ap()
    d_lam = c.dram_in("lam_bc", [128, 512]).ap()
    d_consts = c.dram_in("consts", [128, 6 * 128 + 64]).ap()
    d_cos = c.dram_in("cosT", [128, 1024]).ap()
    d_sin = c.dram_in("sinT", [128, 1024]).ap()
    d_ada_w = c.dram_in("ada_w", [4, D, 3 * D]).ap()
    d_attn_w_in = c.dram_in("attn_w_in", [2, D, 4 * D]).ap()
    d_attn_w_out = c.dram_in("attn_w_out", [2, D, D]).ap()
    d_ckT = c.dram_in("ckT", [2, 8, 128, 256]).ap()
    d_cv = c.dram_in("cv", [2, 8, 256, 128]).ap()
    d_conv_w_in = c.dram_in("conv_w_in", [1, D, 3 * D]).ap()
    d_conv_w_out = c.dram_in("conv_w_out", [1, D, D]).ap()
    d_rwkv_w_in = c.dram_in("rwkv_w_in", [1, D, 4 * D]).ap()
    d_rwkv_w_out = c.dram_in("rwkv_w_out", [1, D, D]).ap()
    d_w1cat = c.dram_in("rw_w1cat", [D, 128]).ap()
    d_a1cat = c.dram_in("rw_a1cat", [D, 128]).ap()
    d_w2cat = c.dram_in("rw_w2cat", [128, D]).ap()
    d_a2cat = c.dram_in("rw_a2cat", [128, D]).ap()
    d_state = c.dram_in("rw_state", [2, 8, 128, 64]).ap()

    o_yT = c.dram_out("yT", [D, NT]).ap()
    o_newk = c.dram_out("newk", [2, 8, 128, 512]).ap()
    o_newv = c.dram_out("newv", [2, 8, 512, 128]).ap()
    o_news = c.dram_out("news", [2, 2, 8, 128, 64]).ap()

    xT = [c.sbuf(f"xT{k}", [128, NT]) for k in range(KC)]
    hT = [c.sbuf(f"hT{k}", [128, NT], BF16) for k in range(KC)]
    ygT = [c.sbuf(f"ygT{k}", [128, NT], BF16) for k in range(KC)]
    S = [c.sbuf(f"S{k}", [128, 1632]) for k in range(DBG.get("nslab", 9))]
    if DBG.get("real_bf"):
        BfR = [c.sbuf(f"BfR{k}", [128, 1792], BF16) for k in range(4)]
    wA = [c.sbuf(f"wA{k}", [128, 8, 256], BF16) for k in range(2)]
    wI = [c.sbuf(f"wI{k}", [128, 8, 512], BF16) for k in range(2)]
    vecs = c.sbuf("vecs_sb", [128, NVEC])
    cond = c.sbuf("cond_sb", [128, 16])
    scT = c.sbuf("scT", [128, 16], BF16)
    consts = c.sbuf("consts_sb", [128, 3 * 128])
    cbf = c.sbuf("consts_bf", [128, 6 * 128 + 64], BF16)
    mod = c.sbuf("mod", [128, 48])
    sc1 = c.sbuf("sc1", [128, 16])
    small = c.sbuf("small", [128, 16])
    ECb = c.sbuf("ECb", [128, 16])
    HFB = [c.sbuf(f"Hfb{q_}", [128, 64]) for q_ in range(2)]
    PTB = [c.sbuf(f"PTb{q_}", [128, 4, 64], BF16) for q_ in range(2)]
    SGR1 = c.sbuf("sgr1", [128, 512], BF16)
    BON1 = c.sbuf("bonus1", [128, 512], BF16)
    HBF0 = c.sbuf("hbf0", [128, 64], BF16)
    E = [c.sbuf(f"E{k}", [128, 512], BF16) for k in range(4)]
    T5 = [c.sbuf(f"T5{k}", [128, 512]) for k in range(6)]
    RSTD = [c.sbuf(f"rstd{k}", [128, 512]) for k in range(2)]
    rotR = Rot(RSTD)

    def bview(buf, name, shape=(128, 1792)):
        return c.view(buf, name, list(shape), BF16)[:]

    pb = [c.psum(f"pb{k}", [128, 512]) for k in range(8)]
    rotS = Rot(pb[0:4])
    rotE = Rot(E)
    rotT = Rot(T5)

    ident_f = consts[:, 0:128]
    ones_f = consts[:, 128:256]
    ident_b = cbf[:, 0:128]
    ones_b = cbf[:, 128:256]
    PR_b = cbf[:, 256:384]

    def vcol(name, j=0, n=1):
        o = VOFF[name] + j
        return vecs[:, o:o + n]

    def MM(ps, lhsT, rhs, start, stop, xr=()):
        c.op("pe", "matmul", ps, lhsT=lhsT, rhs=rhs, start=start, stop=stop, _xr=list(xr))

    def ACT(out, in_, func, **kw):
        c.op("act", "activation", out=out, in_=in_, func=func, **kw)

    def TT(out, in0, in1, op, eng="dve"):
        c.op(eng, "tensor_tensor", out=out, in0=in0, in1=in1, op=op)

    def STT(out, in0, scalar, in1, op0, op1, eng="dve"):
        c.op(eng, "scalar_tensor_tensor", out=out, in0=in0, scalar=scalar, in1=in1, op0=op0, op1=op1)

    def TS(out, in0, s1, s2, op0, op1=None, eng="dve"):
        if op1 is None:
            c.op(eng, "tensor_scalar", out=out, in0=in0, scalar1=s1, scalar2=None, op0=op0)
        else:
            c.op(eng, "tensor_scalar", out=out, in0=in0, scalar1=s1, scalar2=s2, op0=op0, op1=op1)

    def CP(out, in_, eng="dve"):
        c.op(eng, "tensor_copy", out=out, in_=in_)

    def RCP(out, in_):
        c.op("dve", "reciprocal", out=out, in_=in_)

    def wview(dram_w):
        return dram_w.rearrange("(k p) n -> p k n", p=128)

    c.dma("sp", out=vecs[:], in_=d_vecs)
    c.dma("sp", out=cond[:], in_=d_cond)
    c.dma("sp", out=consts[:, 0:256], in_=d_consts[:, 0:256])
    c.dma("sp", out=consts[:, 256:384], in_=d_consts[:, 640:768])
    c.dma("pool", out=cbf[:], in_=d_consts)
    for k in range(KC):
        c.dma("sp", out=xT[k][:], in_=d_xT[k * 128:(k + 1) * 128, :])
    ACT(scT[:], cond[:], AF.Silu)

    mods = [mod, c.sbuf("mod_b", [128, 48])]
    sc1s = [sc1, c.sbuf("sc1_b", [128, 16])]
    gts = [c.sbuf("gate_a", [128, 16]), c.sbuf("gate_b", [128, 16])]
    cur = {"i": 0, "ada": None}

    def adaln_steps(i):
        psada = pb[7]
        md, s1 = mods[i % 2], sc1s[i % 2]
        wv = wview(d_ada_w[i])
        for bi in range(12):
            slot = wA[bi % 2]
            c.dma("pool", out=slot[:], in_=wv[:, :, bi * 256:(bi + 1) * 256])
            for m2 in range(2):
                m = bi * 2 + m2
                for k in range(KC):
                    MM(psada[:, 2 * m:2 * m + 2], slot[:, k, m2 * 128:(m2 + 1) * 128], scT[:, 2 * k:2 * k + 2],
                       start=(k == 0), stop=(k == KC - 1))
            if bi == 7:
                ab = vcol(f"ada_b{i}", 0, 16)
                TT(md[:, 0:32].rearrange("p (m t) -> p m t", t=2), psada[:, 0:32].rearrange("p (m t) -> p m t", t=2),
                   ab.unsqueeze(2).to_broadcast([128, 16, 2]), ALU.add)
                nw = vcol(f"norm_w{i}", 0, 8)
                STT(s1[:].rearrange("p (m t) -> p m t", t=2), md[:, 16:32].rearrange("p (m t) -> p m t", t=2), 1.0,
                    nw.unsqueeze(2).to_broadcast([128, 8, 2]), ALU.add, ALU.mult)
            yield
        ab = vcol(f"ada_b{i}", 16, 8)
        TT(gts[i % 2][:].rearrange("p (m t) -> p m t", t=2), psada[:, 32:48].rearrange("p (m t) -> p m t", t=2),
           ab.unsqueeze(2).to_broadcast([128, 8, 2]), ALU.add)
        yield

    def ada_advance(n):
        g = cur["ada"]
        if g is None:
            return
        for _ in range(n):
            try:
                next(g)
            except StopIteration:
                cur["ada"] = None
                return

    def shift_ap(k, ci):
        return mods[cur["i"] % 2][:, 2 * k + ci:2 * k + ci + 1]

    def sc1_ap(k, ci):
        return sc1s[cur["i"] % 2][:, 2 * k + ci:2 * k + ci + 1]

    def gate_ap(k, ci):
        return gts[cur["i"] % 2][:, 2 * k + ci:2 * k + ci + 1]

    def rms_rstd(tc, eps, src, nchunks=KC, scale=1.0 / D):
        t0, tl, ci = tc
        ps = rotS.next()
        for k in range(nchunks):
            sq = rotT.next()
            TT(sq[:, :tl], src[k][:, t0:t0 + tl], src[k][:, t0:t0 + tl], ALU.mult)
            MM(ps[:, :tl], ones_f, sq[:, :tl], start=(k == 0), stop=(k == nchunks - 1))
        sd = rotT.next()
        ACT(sd[:, :tl], ps[:, :tl], AF.Ln, bias=float(eps), scale=float(scale))
        rstd = rotR.next()
        ACT(rstd[:, :tl], sd[:, :tl], AF.Exp, scale=-0.5)
        return rstd

    def norm_mod(i):
        for tc in TCH:
            t0, tl, ci = tc
            rstd = rms_rstd(tc, NORM_EPS, xT)
            for k in range(KC):
                t = rotT.next()
                TT(t[:, :tl], xT[k][:, t0:t0 + tl], rstd[:, :tl], ALU.mult)
                ACT(hT[k][:, t0:t0 + tl], t[:, :tl], AF.Identity, scale=sc1_ap(k, ci), bias=shift_ap(k, ci))

    def out_proj(d_wout):
        wv = wview(d_wout)
        for n2 in range(2):
            c.dma("pool", out=wI[n2][:], in_=wv[:, :, n2 * 512:(n2 + 1) * 512])
        for m in range(KC):
            slot = wI[m // 4]
            for tc in TCH:
                t0, tl, ci = tc
                ps = rotS.next()
                for k in range(KC):
                    MM(ps[:, :tl], slot[:, k, (m % 4) * 128:(m % 4 + 1) * 128], ygT[k][:, t0:t0 + tl],
                       start=(k == 0), stop=(k == KC - 1))
                STT(xT[m][:, t0:t0 + tl], ps[:, :tl], gate_ap(m, ci), xT[m][:, t0:t0 + tl], ALU.mult, ALU.add)
            ada_advance(2)
        ada_advance(100)

    def proj_fm(ps, wslot, sel, tc):
        t0, tl, ci = tc
        for k in range(KC):
            MM(ps[:, :tl], wslot[:, k, sel], hT[k][:, t0:t0 + tl], start=(k == 0), stop=(k == KC - 1))

    def attn_layer(i, j):
        lam_init = 0.8 - 0.6 * math.exp(-0.3 * i)
        cosT, sinT = S[0], S[1]
        c.dma("sp", out=cosT[:, 0:1024], in_=d_cos)
        c.dma("sp", out=sinT[:, 0:1024], in_=d_sin)
        lam_sb = rotT.next()
        c.dma("sp", out=lam_sb[:], in_=d_lam)
        lo = j * 256
        prod = rotT.next()
        TT(prod[:, 0:64], lam_sb[:, lo:lo + 64], lam_sb[:, lo + 64:lo + 128], ALU.mult)
        TT(prod[:, 64:128], lam_sb[:, lo + 128:lo + 192], lam_sb[:, lo + 192:lo + 256], ALU.mult)
        c.op("dve", "reduce_sum", out=small[:, 0:1], in_=prod[:, 0:64], axis=AX.X)
        c.op("dve", "reduce_sum", out=small[:, 1:2], in_=prod[:, 64:128], axis=AX.X)
        ACT(small[:, 2:4], small[:, 0:2], AF.Exp)
        STT(small[:, 4:5], small[:, 3:4], -lam_init, small[:, 2:3], ALU.add, ALU.subtract)
        neglam = small[:, 4:5]
        TS(small[:, 5:6], vcol(f"subln{j}"), 1.0 - lam_init, None, ALU.mult)
        subw = small[:, 5:6]

        if DBG.get("real_bf"):
            q_bf, k_bf, sg, vkc = [b.t[:] for b in BfR]
        else:
            q_bf, k_bf, sg, vkc = bview(S[2], "qbf"), bview(S[3], "kbf"), bview(S[4], "sgb"), bview(S[5], "vkc")
        if DBG.get("stop") == "attn_setup":
            return
        wv = d_attn_w_in[j].rearrange("(k p) (n c) -> p k n c", p=128, c=1024)
        for h in range(DBG.get("nheads", 8)):
            slot = wI[h % 2]
            sl4 = slot[:].rearrange("p k (n c) -> p k n c", c=128)
            for n4 in range(4):
                c.dma("pool", out=sl4[:, :, n4, :], in_=wv[:, :, n4, h * 128:(h + 1) * 128])
            if not DBG.get("no_cache"):
                c.dma("pool", out=k_bf[:, 1536:1792], in_=d_ckT[j, h])
                c.dma("pool", out=vkc[:, 1536:1792].rearrange("p (t d) -> p t d", d=128),
                      in_=d_cv[j, h].rearrange("(t p) d -> p t d", p=128))
            items = [(which, dst, tc) for which, dst in ((0, q_bf), (1, k_bf)) for tc in TCH]

            def qk_proj(it):
                which, dst, tc = it
                ps = rotS.next()
                proj_fm(ps, slot, slice(which * 128, (which + 1) * 128), tc)
                return ps

            def qk_post(it, ps):
                which, dst, tc = it
                t0, tl, ci = tc
                if ci == 0 and not DBG.get("no_rope"):
                    a_bf = rotE.next()
                    ACT(a_bf[:, :tl], ps[:, :tl], AF.Copy)
                    ps2 = rotS.next()
                    MM(ps2[:, :tl], PR_b, a_bf[:, :tl], start=True, stop=True)
                    t1 = rotT.next()
                    t2 = rotT.next()
                    TT(t1[:, :tl], ps[:, :tl], cosT[:, t0:t0 + tl], ALU.mult)
                    TT(t2[:, :tl], ps2[:, :tl], sinT[:, t0:t0 + tl], ALU.mult)
                    TT(dst[:, t0:t0 + tl], t1[:, :tl], t2[:, :tl], ALU.add)
                else:
                    ACT(dst[:, t0:t0 + tl], ps[:, :tl], AF.Copy)
                    if which == 1 and ci == 1 and not DBG.get("no_out"):
                        kst = rotT.next()
                        CP(kst[:, :tl], ps[:, :tl])
                        c.dma("sp", out=o_newk[j, h], in_=kst[:, :tl])

            ps_cur = qk_proj(items[0])
            for qi in range(len(items)):
                ps_nxt = qk_proj(items[qi + 1]) if qi + 1 < len(items) else None
                qk_post(items[qi], ps_cur)
                ps_cur = ps_nxt
            for t4 in range(3):
                ps = rotS.next()
                for tt in range(4):
                    tok0 = (t4 * 4 + tt) * 128
                    for k in range(KC):
                        MM(ps[:, tt * 128:(tt + 1) * 128], hT[k][:, tok0:tok0 + 128], slot[:, k, 256:384],
                           start=(k == 0), stop=(k == KC - 1))
                CP(vkc[:, t4 * 512:(t4 + 1) * 512], ps[:, :])
                if t4 == 2 and not DBG.get("no_out"):
                    vst = rotT.next()
                    ACT(vst[:, :], ps[:, :], AF.Copy)
                    c.dma("sp", out=o_newv[j, h].rearrange("(t p) d -> p t d", p=128),
                          in_=vst[:, :].rearrange("p (t d) -> p t d", d=128))
            for tc in TCH:
                t0, tl, ci = tc
                ps = rotS.next()
                proj_fm(ps, slot, slice(384, 512), tc)
                ACT(sg[:, t0:t0 + tl], ps[:, :tl], AF.Silu)
            if DBG.get("stop") == "attn_proj":
                continue
            jobs = []
            own = [(k_bf, kt * 128, kt) for kt in range(8)] + [(k_bf, 1536, 12), (k_bf, 1664, 13)]
            jobs.append((0, 512, [e_ + (0, 512) for e_ in own]))
            jobs.append((512, 512, [e_ + (512, 512) for e_ in own]))
            kl = []
            for p in range(2):
                kl += [(k_bf, 1024 + 256 * p + 128 * t, 8 + 2 * p + t, 1024 + 256 * p, 256) for t in range(2)]
            jobs.append((1024, 512, kl))
            for (q0, qn, klist) in jobs:
                pO1, pO2, pZ1, pZ2 = pb[4], pb[5], pb[6], pb[7]
                nk = len(klist)

                def scores(ki):
                    ksrc, kcol, vt, qa, qw = klist[ki]
                    pS1 = rotS.next()
                    pS2 = rotS.next()
                    MM(pS1[:, :qw], ksrc[0:64, kcol:kcol + 128], q_bf[0:64, qa:qa + qw], start=True, stop=True)
                    MM(pS2[:, :qw], ksrc[64:128, kcol:kcol + 128], q_bf[64:128, qa:qa + qw], start=True, stop=True)
                    e1 = rotE.next()
                    e2 = rotE.next()
                    ACT(e1[:, :qw], pS1[:, :qw], AF.Exp, scale=0.125)
                    ACT(e2[:, :qw], pS2[:, :qw], AF.Exp, scale=0.125)
                    return e1, e2

                nxt = scores(0)
                for ki in range(nk):
                    e1, e2 = nxt
                    if ki + 1 < nk:
                        nxt = scores(ki + 1)
                    _, _, vt, qa, qw = klist[ki]
                    vt_ap = vkc[:, vt * 128:(vt + 1) * 128]
                    same = [i_ for i_ in range(nk) if klist[i_][3] == qa]
                    st, sp_ = (ki == same[0]), (ki == same[-1])
                    cs = slice(qa - q0, qa - q0 + qw)
                    MM(pO1[:, cs], vt_ap, e1[:, :qw], start=st, stop=sp_)
                    MM(pZ1[:, cs], ones_b, e1[:, :qw], start=st, stop=sp_)
                    MM(pO2[:, cs], vt_ap, e2[:, :qw], start=st, stop=sp_)
                    MM(pZ2[:, cs], ones_b, e2[:, :qw], start=st, stop=sp_)
                rz1 = rotT.next()
                rz2 = rotT.next()
                oc1 = rotT.next()
                oc2 = rotT.next()
                CP(oc1[:, :qn], pO1[:, :qn])
                ACT(rz1[:, :qn], pZ1[:, :qn], AF.Ln)
                CP(oc2[:, :qn], pO2[:, :qn])
                ACT(rz2[:, :qn], pZ2[:, :qn], AF.Ln)
                ACT(rz1[:, :qn], rz1[:, :qn], AF.Exp, scale=-1.0)
                ACT(rz2[:, :qn], rz2[:, :qn], AF.Exp, scale=-1.0)
                TT(oc1[:, :qn], oc1[:, :qn], rz1[:, :qn], ALU.mult)
                TT(oc2[:, :qn], oc2[:, :qn], rz2[:, :qn], ALU.mult)
                o = rotT.next()
                STT(o[:, :qn], oc2[:, :qn], neglam, oc1[:, :qn], ALU.mult, ALU.add)
                sq = rotT.next()
                TT(sq[:, :qn], o[:, :qn], o[:, :qn], ALU.mult)
                psq = rotS.next()
                MM(psq[:, :qn], ones_f, sq[:, :qn], start=True, stop=True)
                rs = rz1
                ACT(rs[:, :qn], psq[:, :qn], AF.Ln, bias=1e-5, scale=1.0 / 128)
                ACT(rs[:, :qn], rs[:, :qn], AF.Exp, scale=-0.5)
                on = oc1
                TT(on[:, :qn], o[:, :qn], rs[:, :qn], ALU.mult)
                STT(ygT[h][:, q0:q0 + qn], on[:, :qn], subw, sg[:, q0:q0 + qn], ALU.mult, ALU.mult)
        out_proj(d_attn_w_out[j])


    LSC = math.exp(-0.5)

    def rwkv_layer(i, j):
        maskA_b = cbf[:, 384:512]
        maskB_b = cbf[:, 512:640]
        blk_f = consts[:, 256:384]
        xsp = [c.dram_tmp(f"xspill{k}", [128, NT]) for k in range(KC)]
        for k in range(KC):
            c.dma("sp", out=xsp[k].t.ap(), in_=xT[k][:])
        V_ = c.view
        dxT = [V_(S[k // 2], f"dxT{k}", [128, NT], BF16, (k % 2) * 3072) for k in range(KC)]
        t1T = V_(S[4], "t1T", [128, NT], BF16, 0)
        la1T = V_(S[4], "la1T", [128, NT], BF16, 3072)
        w2c = V_(S[5], "w2c", [128, 1024], BF16, 0)
        a2c = V_(S[5], "a2c", [128, 1024], BF16, 2048)
        omka = V_(S[5], "omka", [128, 8], F32, 4096)
        dxtmp = S[6]
        splits = {}

        def VS(parent, name, shape, dtype, off):
            esz = 2 if dtype == BF16 else 4
            nb = int(np.prod(shape[1:])) * esz
            assert off + nb <= parent.nbytes
            for (_, o2, n2) in splits.get(parent.name, []):
                assert off + nb <= o2 or o2 + n2 <= off, "split views must be disjoint"
            c.nview += 1
            t_ = nc.alloc_sbuf_tensor_at(f"{name}_s{c.nview}", list(shape), dtype, offset=parent.off + off)
            b_ = Buf(t_, t_.name, "sbuf")
            b_.off, b_.nbytes = parent.off + off, nb
            c.bufs[t_.name] = b_
            splits.setdefault(parent.name, []).append(((parent, b_, t_), off, nb))
            return t_

        kT = VS(xT[0], "kT", [128, 1024], F32, 0)
        rT = VS(xT[0], "rT", [128, 1024], BF16, 4096)
        ld = VS(xT[1], "ld", [128, 1024], F32, 0)
        vT = VS(xT[1], "vT", [128, 1024], BF16, 4096)
        Lcp = VS(S[6], "Lcp", [128, 1032], F32, 0)
        sgr = VS(S[6], "sgr", [128, 1024], BF16, 4128)
        yacc = VS(xT[2], "yacc", [128, 1024], F32, 0)
        kk = VS(xT[2], "kk", [128, 1024], BF16, 4096)
        kd = [VS(xT[3], f"kd{z}", [128, 1024], BF16, 2048 * z) for z in range(2)]
        bonus = VS(xT[3], "bonus", [128, 1024], BF16, 4096)
        kka = [VS(xT[4], f"kka{z}", [128, 1024], BF16, 2048 * z) for z in range(2)]
        vs = VS(xT[4], "vs", [128, 1024], BF16, 4096)

        def handover(to_children):
            for pname, lst in splits.items():
                parent = lst[0][0][0]
                if to_children:
                    c.op("dve", "memset", parent[:, 0:1], 0.0)
                    for ((_, b_, t_), _, _) in lst:
                        c.op("dve", "memset", t_[:, 0:1], 0.0, _xr=[parent])
                else:
                    for ((_, b_, t_), _, _) in lst:
                        c.op("dve", "memset", t_[:, 0:1], 0.0)
                    c.op("dve", "memset", parent[:, 0:1], 0.0, _xr=[l[0][1] for l in lst])
        KR = V_(xT[6], "KR", [128, 16, 2, 64], BF16, 0)
        BK = V_(xT[7], "BK", [128, 16, 2, 64], BF16, 0)
        BKh = V_(S[7], "BKh", [128, 16, 2, 64], BF16, 0)
        MAB4, TOK, S6 = [], [], []
        for p, b1 in enumerate((S[8], xT[5], wA[0], wA[1])):
            MAB4.append(V_(b1, f"MAB4{p}", [128, 4, 256], BF16, 0))
            TOK.append(V_(b1, f"TOK{p}", [128, 4, 192], BF16, 2048))
            S6.append(V_(b1, f"S6{p}", [128, 4, 64], BF16, 3584))
        PSs = [V_(RSTD[0], "PS0", [128, 4, 128], BF16, 0), V_(E[0], "PS1", [128, 4, 128], BF16, 0)]
        PSS, PSSB = [], []
        for q_, base_ in enumerate((RSTD[0], E[0])):
            t_ = nc.alloc_sbuf_tensor_at(f"PSS{q_}", [128, 4, 128], BF16, offset=base_.off)
            b_ = Buf(t_, f"PSS{q_}", "sbuf")
            b_.off, b_.nbytes = base_.off, 1024
            c.bufs[t_.name] = b_
            PSS.append(t_)
            PSSB.append(b_)
            c.op("dve", "memset", base_[:, 32:33], 0.0)
            c.op("dve", "memset", t_[:, 0, 64:65], 0.0, _xr=[base_])
        PTs = [PTB[0].t, PTB[1].t]
        SMALL = []
        for q_, (smz, smu, smh) in enumerate(((E[3], E[2], HBF0), (E[0], E[1], RSTD[1]))):
            SMALL.append((V_(smz, f"Zs{q_}", [128, 64], BF16, 0), V_(smu, f"Us{q_}", [128, 64], BF16, 0),
                          V_(smh, f"Hbf{q_}", [128, 64], BF16, 0), HFB[q_]))
        EC = ECb
        ident2_b = cbf[:, 768:832]

        dxtmps = [S[6], S[7]]
        for k in range(KC):
            dxtmp = dxtmps[k % 2]
            en = "pool" if k % 4 == 3 else "dve"
            TT(dxtmp[:, 1:NT - 1], hT[k][:, 0:NT - 2], hT[k][:, 2:NT], ALU.add, eng=en)
            for (s0, L) in SEGS:
                ACT(dxtmp[:, s0:s0 + 1], hT[k][:, s0 + 1:s0 + 2], AF.Copy)
                ACT(dxtmp[:, s0 + L - 1:s0 + L], hT[k][:, s0 + L - 2:s0 + L - 1], AF.Copy)
            STT(dxT[k][:, :], dxtmp[:, 0:NT], 0.5, hT[k][:, :], ALU.mult, ALU.subtract)
        c.dma("pool", out=w2c[:, :], in_=d_w2cat)
        c.dma("pool", out=a2c[:, :], in_=d_a2cat)
        TS(omka[:, :], vcol("k_a", 0, 8), -1.0, 1.0, ALU.mult, ALU.add)
        for li, (dsrc, mun, dst, func) in enumerate(((d_w1cat, "mu4", t1T, AF.Tanh), (d_a1cat, "mu5", la1T, AF.Copy))):
            slot = wA[li]
            c.dma("pool", out=slot[:, :, 0:128], in_=dsrc.rearrange("(k p) n -> p k n", p=128))
            TT(slot[:, :, 128:256], slot[:, :, 0:128], vcol(mun, 0, 8).unsqueeze(2).to_broadcast([128, 8, 128]), ALU.mult)
            for tc in TCH:
                t0, tl, ci = tc
                ps = rotS.next()
                for k in range(KC):
                    MM(ps[:, :tl], slot[:, k, 0:128], hT[k][:, t0:t0 + tl], start=(k == 0), stop=False)
                for k in range(KC):
                    MM(ps[:, :tl], slot[:, k, 128:256], dxT[k][:, t0:t0 + tl], start=False, stop=(k == KC - 1))
                ACT(dst[:, t0:t0 + tl], ps[:, :tl], func)

        def run_interleaved(lists):
            n = max(len(l) for l in lists) if lists else 0
            for q in range(n):
                for l in lists:
                    if q < len(l):
                        l[q]()

        sgr1 = SGR1.t
        bonus1 = BON1.t
        wv = d_rwkv_w_in[j].rearrange("(k p) (n c) -> p k n c", p=128, c=1024)
        PASSES = [(0, 1024, [(0, 1024)], [(0, 512), (512, 512)], 0), (1024, 512, [(0, 256), (256, 256)], [(0, 512)], 1)]

        def scan_ap(arr, z, a0, b0, seg, rows=slice(0, 128)):
            s0, L = seg
            if z == 0:
                return arr[rows, a0:b0]
            lo = s0 + (s0 + L - b0)
            hi = s0 + (s0 + L - a0)
            return arr[rows, lo:hi][:, ::-1]

        W, Ws = wI[0], wI[1]
        W4 = W[:].rearrange("p k (n c) -> p k n c", c=128)
        Ws4 = Ws[:].rearrange("p k (n c) -> p k n c", c=128)

        def load_weights(hp_):
            for n4 in range(4):
                c.dma("pool", out=W4[:, :, n4, :], in_=wv[:, :, n4, hp_ * 128:(hp_ + 1) * 128])
                TT(Ws4[:, :, n4, :], W4[:, :, n4, :], vcol(f"mu{n4}", 0, 8).unsqueeze(2).to_broadcast([128, 8, 128]), ALU.mult)

        NHP = DBG.get("nhp", 8)
        handover(True)
        load_weights(0)
        for hp in range(NHP):
            def make_pass(tok0, TL, segs, tcs, ci, sgr, bonus):
                nch = TL // 64
                def base_proj1(lo, n, n4):
                    g0 = tok0 + lo
                    if True:
                        ps = rotS.next()
                        for k in range(KC):
                            MM(ps[:, :n], W[:, k, n4 * 128:(n4 + 1) * 128], hT[k][:, g0:g0 + n], start=(k == 0), stop=False)
                        for k in range(KC):
                            MM(ps[:, :n], Ws[:, k, n4 * 128:(n4 + 1) * 128], dxT[k][:, g0:g0 + n], start=False, stop=(k == KC - 1))
                        if n4 == 0:
                            ACT(rT[:, lo:lo + n], ps[:, :n], AF.Copy)
                        elif n4 == 1:
                            CP(kT[:, lo:lo + n], ps[:, :n])
                        elif n4 == 2:
                            ACT(vT[:, lo:lo + n], ps[:, :n], AF.Copy)
                        else:
                            ACT(sgr[:, lo:lo + n], ps[:, :n], AF.Silu)

                def base_chain(lo, n):
                    g0 = tok0 + lo
                    kkf, sq, mx = T5[0], T5[1], T5[2]
                    TS(kkf[:, :n], kT[:, lo:lo + n], vcol("k_k", hp), None, ALU.mult)
                    TT(sq[:, :n], kkf[:, :n], kkf[:, :n], ALU.mult)
                    ps = rotS.next()
                    MM(ps[:, :n], blk_f, sq[:, :n], start=True, stop=True)
                    TS(mx[:, :n], ps[:, :n], 1e-24, None, ALU.max)
                    ACT(mx[:, :n], mx[:, :n], AF.Ln)
                    ACT(mx[:, :n], mx[:, :n], AF.Exp, scale=-0.5)
                    TT(kk[:, lo:lo + n], kkf[:, :n], mx[:, :n], ALU.mult)
                    for z in range(2):
                        az, tmp = T5[3 + z], T5[5]
                        ps = rotS.next()
                        MM(ps[:, :n], a2c[z * 64:(z + 1) * 64, hp * 128:(hp + 1) * 128], la1T[z * 64:(z + 1) * 64, g0:g0 + n],
                           start=True, stop=True)
                        ACT(az[:, :n], ps[:, :n], AF.Sigmoid, bias=vcol(f"a0_{z}", hp))
                        TS(tmp[:, :n], az[:, :n], vcol("k_a", hp), omka[:, hp:hp + 1], ALU.mult, ALU.add)
                        TT(kd[z][:, lo:lo + n], tmp[:, :n], kT[:, lo:lo + n], ALU.mult)
                        TT(kka[z][:, lo:lo + n], kk[:, lo:lo + n], az[:, :n], ALU.mult)
                    tmp, rk = T5[0], T5[1]
                    TT(tmp[:, :n], kd[0][:, lo:lo + n], kd[1][:, lo:lo + n], ALU.add)
                    STT(rk[:, :n], rT[:, lo:lo + n], vcol("r_k", hp), tmp[:, :n], ALU.mult, ALU.mult)
                    ps = rotS.next()
                    MM(ps[:, :n], blk_f, rk[:, :n], start=True, stop=True)
                    TT(bonus[:, lo:lo + n], ps[:, :n], vT[:, lo:lo + n], ALU.mult)

                def base_proj(lo, n):
                    for n4 in range(4):
                        base_proj1(lo, n, n4)

                def base_all():
                    base_proj(*tcs[0])
                    for ti in range(len(tcs)):
                        if ti + 1 < len(tcs):
                            base_proj(*tcs[ti + 1])
                        base_chain(*tcs[ti])

                def base_steps():
                    st = []
                    for (lo_, n_) in tcs:
                        for n4 in range(4):
                            st.append(lambda lo_=lo_, n_=n_, n4=n4: base_proj1(lo_, n_, n4))
                        st.append(lambda lo_=lo_, n_=n_: base_chain(lo_, n_))
                    return st


                def scan_z(z):
                    for (lo, n) in tcs:
                        g0 = tok0 + lo
                        ps = rotS.next()
                        MM(ps[:, :n], w2c[z * 64:(z + 1) * 64, hp * 128:(hp + 1) * 128], t1T[z * 64:(z + 1) * 64, g0:g0 + n],
                           start=True, stop=True)
                        for seg in segs:
                            s0, L = seg
                            a0, b0 = max(lo, s0), min(lo + n, s0 + L)
                            if a0 >= b0:
                                continue
                            if z == 0:
                                dst = ld[:, a0:b0]
                            else:
                                dst = ld[:, s0 + (s0 + L - b0):s0 + (s0 + L - a0)][:, ::-1]
                            ACT(dst, ps[:, a0 - lo:b0 - lo], AF.Sigmoid, bias=vcol(f"w0_{z}", hp))
                    if z == 1:
                        for (s0, L) in segs:
                            CP(vs[:, s0:s0 + L], vT[:, s0:s0 + L][:, ::-1], eng="pool")
                    vsrc = vT if z == 0 else vs
                    c.op("dve", "memset", Lcp[:, 0:1], 0.0)
                    c.op("dve", "tensor_tensor_scan", out=Lcp[:, 1:TL + 1], data0=ones_f[:, 0:1].to_broadcast([128, TL]),
                         data1=ld[:, 0:TL], initial=0.0, op0=ALU.mult, op1=ALU.add)
                    for p0 in range(0, TL, 512):
                        Lloc, Ep, Em, Epv, Eh = T5[0], T5[1], T5[2], T5[3], T5[4]
                        v3 = lambda ap: ap.rearrange("p (c t) -> p c t", t=64)
                        TT(v3(Lloc[:, :]), v3(Lcp[:, 1 + p0:1 + p0 + 512]),
                           Lcp[:, p0:p0 + 512:64].unsqueeze(2).to_broadcast([128, 8, 64]), ALU.subtract)
                        ACT(Ep[:, :], Lloc[:, :], AF.Exp, scale=-LSC)
                        ACT(Em[:, :], Lloc[:, :], AF.Exp, scale=LSC)
                        TT(Epv[:, :], Lloc[:, :], ld[:, p0:p0 + 512], ALU.subtract)
                        ACT(Epv[:, :], Epv[:, :], AF.Exp, scale=-LSC)
                        TT(v3(Eh[:, :]), v3(Lloc[:, :])[:, :, 63:64].to_broadcast([128, 8, 64]), v3(Lloc[:, :]), ALU.subtract)
                        ACT(Eh[:, :], Eh[:, :], AF.Exp, scale=-LSC)
                        c0 = p0 // 64
                        ACT(EC[:, c0:c0 + 8], v3(Lloc[:, :])[:, :, 63], AF.Exp, scale=-LSC)
                        for seg in segs:
                            s0, L = seg
                            a0, b0 = max(p0, s0), min(p0 + 512, s0 + L)
                            if a0 >= b0:
                                continue
                            ca, cb = a0 // 64, b0 // 64
                            e = lambda buf: v3(buf[:, a0 - p0:b0 - p0])
                            sa = lambda arr: v3(scan_ap(arr, z, a0, b0, seg))
                            TT(KR[:, ca:cb, 0, :], sa(kk), e(Epv), ALU.mult)
                            TT(KR[:, ca:cb, 1, :], sa(rT), e(Ep), ALU.mult)
                            TT(BK[:, ca:cb, 0, :], sa(kka[z]), e(Em), ALU.mult)
                            TT(BK[:, ca:cb, 1, :], sa(kd[z]), e(Em), ALU.mult)
                            TT(BKh[:, ca:cb, 0, :], sa(kka[z]), e(Eh), ALU.mult)
                            TT(BKh[:, ca:cb, 1, :], sa(kd[z]), e(Eh), ALU.mult)

                    groups = []
                    for si, (s0, L) in enumerate(segs):
                        for gq in range(L // 256):
                            groups.append((si, s0, L, (s0 + gq * 256) // 64, gq == 0, gq == L // 256 - 1))
                    PAS = (slice(0, 64), slice(64, 128))

                    def MM2(bank, cols, lhs_fn, rhs_fn):
                        for a in range(2):
                            pa = PAS[a]
                            MM(bank[pa, cols], lhs_fn(pa, a), rhs_fn(pa, a), start=True, stop=True)

                    def yacc_ap(seg_s0, seg_L, p_lo, p_hi):
                        if z == 0:
                            return yacc[:, p_lo:p_hi]
                        return yacc[:, seg_s0 + (seg_s0 + seg_L - p_hi):seg_s0 + (seg_s0 + seg_L - p_lo)][:, ::-1]

                    def pre_stages(gi, strm):
                        si, s0, L, c0, first, last = groups[gi]
                        p = gi % 4
                        PS, PT = PSs[strm], PTs[strm]
                        PSs_, PSb_ = PSS[strm], PSSB[strm]
                        B0, B1 = pb[2 * strm], pb[2 * strm + 1]
                        st = []
                        for ii in range(4):
                            def sb(ii=ii):
                                cc = c0 + ii
                                bank = (B0, B1)[ii % 2]
                                krf = lambda pa, a: KR[pa, cc, :, :].rearrange("p a b -> p (a b)")
                                MM2(bank, slice(0, 128), lambda pa, a: BK[pa, cc, 0, :], krf)
                                MM2(bank, slice(128, 256), lambda pa, a: BK[pa, cc, 1, :], krf)
                                MM2(bank, slice(256, 320), lambda pa, a: KR[pa, cc, 0, :], lambda pa, a: BK[pa, cc, 0, :])
                                TT(MAB4[p][:, ii, :].rearrange("p (h t) -> p h t", t=128),
                                   bank[:, 0:256].rearrange("p (h t) -> p h t", t=128),
                                   maskA_b.unsqueeze(1).to_broadcast([128, 2, 128]), ALU.mult)
                                TT(PT[:, ii, :], bank[:, 256:320], maskB_b[:, 0:64], ALU.mult)
                            st.append(sb)

                        def sinit():
                            TT(PSs_[:, :, 64:128], ident2_b.unsqueeze(1).to_broadcast([128, 4, 64]), MAB4[p][:, :, 0:64], ALU.subtract)
                        st.append(sinit)
                        for lvl in range(6):
                            def sl(lvl=lvl):
                                bP, bT = B0, B1
                                bP3 = bP[:, :].rearrange("p (c t) -> p c t", t=128)
                                if lvl == 0:
                                    for ii in range(4):
                                        MM2(bP, slice(ii * 128, ii * 128 + 64), lambda pa, a: PT[pa, ii, :], lambda pa, a: MAB4[p][pa, ii, 0:64])
                                    for ii in range(4):
                                        MM2(bT, slice(ii * 64, (ii + 1) * 64), lambda pa, a: MAB4[p][pa, ii, 0:64], lambda pa, a: PT[pa, ii, :])
                                else:
                                    for ii in range(4):
                                        for a in range(2):
                                            pa = PAS[a]
                                            MM(bP[pa, ii * 128:(ii + 1) * 128], PT[pa, ii, :], PS[pa, ii, :], start=True, stop=True, xr=[PSb_])
                                    if lvl < 5:
                                        for ii in range(4):
                                            MM2(bT, slice(ii * 64, (ii + 1) * 64), lambda pa, a: PS[pa, ii, 0:64], lambda pa, a: PT[pa, ii, :])
                                    TT(PSs_[:, :, 64:128], PSs_[:, :, 64:128], bP3[:, :, 64:128], ALU.add)
                                if lvl < 5:
                                    ACT(PS[:, :, 0:64], bP3[:, :, 0:64], AF.Copy)
                                    CP(PT[:, :, :], bT[:, 0:256].rearrange("p (c t) -> p c t", t=64))
                            st.append(sl)

                        def s6():
                            ACT(S6[p][:, :, :], PSs_[:, :, 64:128], AF.Copy)
                        st.append(s6)
                        for h2 in range(2):
                            def stok(h2=h2):
                                bank = (B0, B1)[h2]
                                for i2 in range(2):
                                    ii = h2 * 2 + i2
                                    cc = c0 + ii
                                    col = i2 * 192
                                    idf = lambda pa, a: ident_b[pa, a * 64:(a + 1) * 64]
                                    MM2(bank, slice(col, col + 64), lambda pa, a: BKh[pa, cc, 0, :], idf)
                                    MM2(bank, slice(col + 64, col + 128), lambda pa, a: BKh[pa, cc, 1, :], idf)
                                    MM2(bank, slice(col + 128, col + 192), lambda pa, a: vsrc[pa, cc * 64:(cc + 1) * 64], idf)
                                ACT(TOK[p][:, 2 * h2:2 * h2 + 2, :], bank[:, 0:384].rearrange("p (c t) -> p c t", t=192), AF.Copy)
                            st.append(stok)
                        return st

                    def MMg(bank, cols, lhs_fn, rhs_fn, start, stop):
                        for a in range(2):
                            pa = PAS[a]
                            MM(bank[pa, cols], lhs_fn(pa, a), rhs_fn(pa, a), start=start, stop=stop)

                    def rec_stages(gi, chain=0):
                        si, s0, L, c0, first, last = groups[gi]
                        p = gi % 4
                        st = []
                        R0, R1, R2, R3 = (pb[4], pb[5], pb[6], pb[7]) if chain == 0 else (pb[0], pb[1], pb[2], pb[3])
                        Zs, Us, Hbf, Hf = SMALL[chain]
                        c64 = slice(0, 64)
                        if first:
                            def sinit():
                                if ci == 0:
                                    c.dma("sp", out=Hf[:, :], in_=d_state[z, hp])
                                else:
                                    c.op("dve", "memset", Hf[:, :], 0.0)
                                ACT(Hbf[:, :], Hf[:, :], AF.Copy)
                            st.append(sinit)
                        for ii in range(4):
                            cc = c0 + ii

                            def sz(ii=ii, cc=cc):
                                V_f = lambda pa, a: TOK[p][pa, ii, 128:192]
                                MMg(R0, c64, lambda pa, a: MAB4[p][pa, ii, 128:192], V_f, True, False)
                                MMg(R0, c64, lambda pa, a: KR[pa, cc, 0, :], lambda pa, a: Hbf[pa, :], False, True)
                                MMg(R3, c64, V_f, lambda pa, a: MAB4[p][pa, ii, 192:256], True, False)
                                MMg(R3, c64, lambda pa, a: Hbf[pa, :], lambda pa, a: KR[pa, cc, 1, :], False, False)
                                MMg(R2, c64, lambda pa, a: TOK[p][pa, ii, 64:128], V_f, True, False)
                                CP(Zs[:, :], R0[:, 0:64])

                            def su(ii=ii, cc=cc):
                                MMg(R1, c64, lambda pa, a: S6[p][pa, ii, :], lambda pa, a: Zs[pa, :], True, True)
                                TS(Us[:, :], R1[:, 0:64], -1.0, None, ALU.mult)

                            def shy(ii=ii, cc=cc):
                                MMg(R2, c64, lambda pa, a: TOK[p][pa, ii, 0:64], lambda pa, a: Us[pa, :], False, True)
                                MMg(R3, c64, lambda pa, a: Us[pa, :], lambda pa, a: MAB4[p][pa, ii, 64:128], False, True)
                                STT(Hbf[:, :], Hf[:, :], ECb[:, cc:cc + 1], R2[:, 0:64], ALU.mult, ALU.add)
                                STT(Hf[:, :], Hf[:, :], ECb[:, cc:cc + 1], R2[:, 0:64], ALU.mult, ALU.add)
                                ya = yacc_ap(s0, L, cc * 64, cc * 64 + 64)
                                if z == 0:
                                    ACT(ya, R3[:, 0:64], AF.Copy)
                                else:
                                    TT(ya, R3[:, 0:64], ya, ALU.add)
                            st.extend([sz, su, shy])
                        if last and ci == 1:
                            def sfin():
                                c.dma("sp", out=o_news[si, z, hp], in_=Hf[:, :])
                            st.append(sfin)
                        return st

                    ng = len(groups)
                    rounds = [(r, r + 1) for r in range(0, ng, 2)]
                    prev = None
                    for (ga, gb) in rounds:
                        lists = [pre_stages(ga, 0), pre_stages(gb, 1)]
                        if prev is not None:
                            lists.append(rec_stages(prev[0]) + rec_stages(prev[1]))
                        run_interleaved(lists)
                        prev = (ga, gb)
                    if groups[prev[0]][0] != groups[prev[1]][0]:
                        return [rec_stages(prev[0], 0), rec_stages(prev[1], 1)]
                    return [rec_stages(prev[0]) + rec_stages(prev[1])]

                def gn():
                    for (lo, n) in tcs:
                        g0 = tok0 + lo
                        sq, mean, msq, d1, yn = T5[0], T5[1], T5[2], T5[3], T5[4]
                        TT(sq[:, :n], yacc[:, lo:lo + n], yacc[:, lo:lo + n], ALU.mult)
                        ps1 = rotS.next()
                        ps2 = rotS.next()
                        MM(ps1[:, :n], blk_f, yacc[:, lo:lo + n], start=True, stop=True)
                        MM(ps2[:, :n], blk_f, sq[:, :n], start=True, stop=True)
                        ACT(mean[:, :n], ps1[:, :n], AF.Copy, scale=1.0 / 64)
                        TT(msq[:, :n], mean[:, :n], mean[:, :n], ALU.mult)
                        STT(msq[:, :n], ps2[:, :n], 1.0 / 64, msq[:, :n], ALU.mult, ALU.subtract)
                        ACT(msq[:, :n], msq[:, :n], AF.Ln, bias=64e-5, scale=1.0)
                        ACT(msq[:, :n], msq[:, :n], AF.Exp, scale=-0.5)
                        TT(d1[:, :n], yacc[:, lo:lo + n], mean[:, :n], ALU.subtract)
                        TT(d1[:, :n], d1[:, :n], msq[:, :n], ALU.mult)
                        ACT(yn[:, :n], d1[:, :n], AF.Identity, scale=vcol("rln_w", hp), bias=vcol("rln_b", hp))
                        TT(yn[:, :n], yn[:, :n], bonus[:, lo:lo + n], ALU.add)
                        TT(ygT[hp][:, g0:g0 + n], yn[:, :n], sgr[:, lo:lo + n], ALU.mult)
                return base_all, base_steps, scan_z, gn

            P0 = make_pass(*PASSES[0], sgr, bonus)
            P1 = make_pass(*PASSES[1], sgr1, bonus1)
            P0[0]()
            run_interleaved(P0[2](0))
            t_ = P0[2](1)
            if DBG.get('overlap_base', 0):
                bs_ = P1[1]()
                sp_ = DBG.get('base_spacing', 4)
                spaced = []
                for b_ in bs_:
                    spaced.extend([(lambda: None)] * (sp_ - 1) + [b_])
                run_interleaved(t_ + [spaced])
            else:
                run_interleaved(t_)
            P0[3]()
            if not DBG.get('overlap_base', 0):
                P1[0]()
            if hp + 1 < NHP:
                load_weights(hp + 1)
            run_interleaved(P1[2](0))
            run_interleaved(P1[2](1))
            P1[3]()
        handover(False)
        for q_, base_ in enumerate((RSTD[0], E[0])):
            c.op("dve", "memset", PSS[q_][:, 0, 64:65], 0.0)
            c.op("dve", "memset", base_[:, 32:33], 0.0, _xr=[PSSB[q_]])
        for k in range(KC):
            c.dma("sp", out=xT[k][:], in_=xsp[k].t.ap())
        out_proj(d_rwkv_w_out[j])

    def conv_layer(i, j):
        zc = S[0:8]
        zpad = c.view(S[8], "zpadb", [128, 1632], BF16, 0)
        diags = [c.view(wA[0], "diagA", [128, 16, 128], BF16, 0), c.view(wA[1], "diagB", [128, 15, 128], BF16, 0)]
        NU = PTOT - 2 * PADW
        c.op("dve", "memset", zpad[:, 0:PTOT], 0.0)
        wv = d_conv_w_in[j].rearrange("(k p) (n c) -> p k n c", p=128, c=1024)
        pieces_of_tc = {0: [(0, 0, 512)], 1: [(0, 512, 512)], 2: [(1, 0, 256), (2, 0, 256)]}
        nchunks = [(0, 512), (512, 512), (1024, 512), (1536, NU - 1536)]
        for cc in range(KC):
            slot = wI[cc % 2]
            sl4 = slot[:].rearrange("p k (n c) -> p k n c", c=128)
            for n3 in range(3):
                c.dma("pool", out=sl4[:, :, n3, :], in_=wv[:, :, n3, cc * 128:(cc + 1) * 128])
            for k in range(CONVW):
                o = VOFF["cdw_w"] + cc * CONVW + k
                dg = diags[0][:, k, :] if k < 16 else diags[1][:, k - 16, :]
                c.op("dve", "tensor_scalar", out=dg, in0=ident_b, scalar1=vecs[:, o:o + 1], scalar2=None, op0=ALU.mult)
            for tci, tc in enumerate(TCH):
                t0, tl, ci = tc
                psa = rotS.next()
                proj_fm(psa, slot, slice(0, 128), tc)
                psb = rotS.next()
                proj_fm(psb, slot, slice(128, 256), tc)
                sb = rotT.next()
                ACT(sb[:, :tl], psb[:, :tl], AF.Sigmoid)
                for (sg_, l0, n) in pieces_of_tc[tci]:
                    src0 = SEGS[sg_][0] + l0 - t0
                    dst0 = POFF[sg_] + PADW + l0
                    TT(zpad[:, dst0:dst0 + n], psa[:, src0:src0 + n], sb[:, src0:src0 + n], ALU.mult)
            for tci, tc in enumerate(TCH):
                t0, tl, ci = tc
                psg = rotS.next()
                proj_fm(psg, slot, slice(256, 384), tc)
                ACT(ygT[cc][:, t0:t0 + tl], psg[:, :tl], AF.Silu)
            for ni, (u0, n) in enumerate(nchunks):
                psc = pb[4 + ni]
                for k in range(CONVW):
                    dg = diags[0][:, k, :] if k < 16 else diags[1][:, k - 16, :]
                    MM(psc[:, :n], dg, zpad[:, u0 + k:u0 + k + n], start=(k == 0), stop=(k == CONVW - 1))
                ACT(zc[cc][:, u0:u0 + n], psc[:, :n], AF.Identity, bias=vcol("cdw_b", cc))
        for (sg_, l0, n) in [(0, 0, 512), (0, 512, 512), (1, 0, 256), (2, 0, 256)]:
            pos = POFF[sg_] + l0
            tok = SEGS[sg_][0] + l0
            ps1 = rotS.next()
            ps2 = rotS.next()
            for cc in range(KC):
                sq = rotT.next()
                TT(sq[:, :n], zc[cc][:, pos:pos + n], zc[cc][:, pos:pos + n], ALU.mult, eng="pool")
                MM(ps1[:, :n], ones_f, zc[cc][:, pos:pos + n], start=(cc == 0), stop=(cc == KC - 1))
                MM(ps2[:, :n], ones_f, sq[:, :n], start=(cc == 0), stop=(cc == KC - 1))
            mean = rotR.next()
            ACT(mean[:, :n], ps1[:, :n], AF.Copy, scale=1.0 / D)
            msq = rotT.next()
            TT(msq[:, :n], mean[:, :n], mean[:, :n], ALU.mult)
            var = rotT.next()
            STT(var[:, :n], ps2[:, :n], 1.0 / D, msq[:, :n], ALU.mult, ALU.subtract)
            sd = rotT.next()
            ACT(sd[:, :n], var[:, :n], AF.Ln, bias=1e-5, scale=1.0)
            rstd = rotR.next()
            ACT(rstd[:, :n], sd[:, :n], AF.Exp, scale=-0.5)
            for cc in range(KC):
                d1 = rotT.next()
                TT(d1[:, :n], zc[cc][:, pos:pos + n], mean[:, :n], ALU.subtract)
                d2 = rotT.next()
                TT(d2[:, :n], d1[:, :n], rstd[:, :n], ALU.mult)
                s3 = rotT.next()
                ACT(s3[:, :n], d2[:, :n], AF.Silu, scale=vcol("cln_w", cc), bias=vcol("cln_b", cc))
                TT(ygT[cc][:, tok:tok + n], s3[:, :n], ygT[cc][:, tok:tok + n], ALU.mult)
        out_proj(d_conv_w_out[j])

    layer_list = DBG.get("layers", list(range(n_layers)))
    for li, i in enumerate(layer_list):
        kind, j = i % 3, i // 3
        cur["i"] = i
        if li == 0:
            g0_ = adaln_steps(i)
            for _ in range(8):
                next(g0_)
            norm_mod(i)
            for _ in g0_:
                pass
        cur["ada"] = adaln_steps(layer_list[li + 1]) if li + 1 < len(layer_list) else None
        if DBG.get("stop") == "adaln":
            break
        if li > 0:
            norm_mod(i)
        if DBG.get("stop") == "norm":
            break
        if kind == 0:
            attn_layer(i, j)
        elif kind == 1:
            rwkv_layer(i, j)
        else:
            conv_layer(i, j)

    if debug_x:
        for k in range(KC):
            c.dma("sp", out=o_yT[k * 128:(k + 1) * 128, :], in_=xT[k][:])
    else:
        for tc in TCH:
            t0, tl, ci = tc
            rstd = rms_rstd(tc, NORM_EPS, xT)
            for k in range(KC):
                t = rotT.next()
                TT(t[:, :tl], xT[k][:, t0:t0 + tl], rstd[:, :tl], ALU.mult)
                yb = rotT.next()
                ACT(yb[:, :tl], t[:, :tl], AF.Identity, scale=vcol("final_w", k))
                c.dma("sp", out=o_yT[k * 128:(k + 1) * 128, t0:t0 + tl], in_=yb[:, :tl])
    print("SBUF top", c.sb_ptr, "of", SB_END)
    c.emit()
    return nc, c


def rope_tables():
    T = 1024
    t = np.arange(T)
    row = (t // 64).astype(np.float32)
    col = (t % 64).astype(np.float32)
    inv_freq = (10000.0 ** (-np.arange(0, 32, 2, dtype=np.float32) / 32)).astype(np.float32)
    cosT = np.zeros((128, T), np.float32)
    sinT = np.zeros((128, T), np.float32)
    for d in range(128):
        dd = d % 64
        pos = row if dd < 32 else col
        f = dd % 16
        ang = (pos * inv_freq[f]).astype(np.float32)
        cosT[d] = np.cos(ang)
        sinT[d] = np.sin(ang)
    return cosT, sinT


def const_tables():
    ident = np.eye(128, dtype=np.float32)
    ones = np.ones((128, 128), np.float32)
    PR = np.zeros((128, 128), np.float32)
    for m in range(128):
        if m % 32 < 16:
            PR[m + 16, m] = -1.0
        else:
            PR[m - 16, m] = 1.0
    s = np.arange(128)[:, None] % 64
    t = np.arange(64)[None, :]
    maskA = np.concatenate([(s < t), (s <= t)], axis=1).astype(np.float32)
    maskB = np.concatenate([(s > t), (s >= t)], axis=1).astype(np.float32)
    blk = np.zeros((128, 128), np.float32)
    blk[0:64, 0:64] = 1.0
    blk[64:128, 64:128] = 1.0
    ident2 = np.concatenate([np.eye(64, dtype=np.float32), np.eye(64, dtype=np.float32)], axis=0)
    return np.concatenate([ident, ones, PR, maskA, maskB, blk, ident2], axis=1)


def make_in_maps(inp):
    f = lambda a: np.ascontiguousarray(np.asarray(a, np.float32))
    vecs_common = np.zeros((128, NVEC), np.float32)

    def put(name, arr2d):
        o = VOFF[name]
        vecs_common[:, o:o + arr2d.shape[1]] = arr2d

    for i in range(DEPTH):
        put(f"norm_w{i}", fm(inp["norm_w"][i]))
        put(f"ada_b{i}", np.asarray(inp["ada_b"][i], np.float32).reshape(24, 128).T)
    put("final_w", fm(inp["final_norm_w"]))
    put("subln0", np.asarray(inp["attn_subln_w"][0], np.float32).reshape(128, 1))
    put("subln1", np.asarray(inp["attn_subln_w"][1], np.float32).reshape(128, 1))
    for k in range(6):
        put(f"mu{k}", fm(inp["rwkv_mu"][0, k]))
    for z in range(2):
        put(f"w0_{z}", fm(inp["rwkv_w0"][0, z]))
        put(f"a0_{z}", fm(inp["rwkv_a0"][0, z]))
    put("k_k", fm(inp["rwkv_k_k"][0]))
    put("k_a", fm(inp["rwkv_k_a"][0]))
    put("r_k", fm(np.asarray(inp["rwkv_r_k"][0]).reshape(-1)))
    put("rln_w", fm(inp["rwkv_ln_w"][0]))
    put("rln_b", fm(inp["rwkv_ln_b"][0]))
    put("cdw_b", fm(inp["conv_dw_b"][0]))
    put("cln_w", fm(inp["conv_ln_w"][0]))
    put("cln_b", fm(inp["conv_ln_b"][0]))
    dw = np.asarray(inp["conv_dw_w"][0], np.float32)
    dwl = dw.reshape(CONVW, 8, 128).transpose(2, 1, 0).reshape(128, 8 * CONVW)
    put("cdw_w", dwl)

    lam_bc = np.ascontiguousarray(np.broadcast_to(np.asarray(inp["attn_lambda"], np.float32).reshape(1, 512), (128, 512)))
    consts = const_tables()
    cosT, sinT = rope_tables()
    w1cat = f(np.concatenate([inp["rwkv_w1"][0, 0], inp["rwkv_w1"][0, 1]], axis=1))
    a1cat = f(np.concatenate([inp["rwkv_a1"][0, 0], inp["rwkv_a1"][0, 1]], axis=1))
    w2cat = f(np.concatenate([inp["rwkv_w2"][0, 0], inp["rwkv_w2"][0, 1]], axis=0))
    a2cat = f(np.concatenate([inp["rwkv_a2"][0, 0], inp["rwkv_a2"][0, 1]], axis=0))
    shared = {
        "vecs": vecs_common, "lam_bc": lam_bc, "consts": consts, "cosT": cosT, "sinT": sinT,
        "ada_w": f(inp["ada_w"]), "attn_w_in": f(inp["attn_w_in"]), "attn_w_out": f(inp["attn_w_out"]),
        "conv_w_in": f(inp["conv_w_in"]), "conv_w_out": f(inp["conv_w_out"]),
        "rwkv_w_in": f(inp["rwkv_w_in"]), "rwkv_w_out": f(inp["rwkv_w_out"]),
        "rw_w1cat": w1cat, "rw_a1cat": a1cat, "rw_w2cat": w2cat, "rw_a2cat": a2cat,
    }
    maps = []
    xs, xp = np.asarray(inp["x_sample"], np.float32), np.asarray(inp["x_prompt"], np.float32)
    for b in range(8):
        m = dict(shared)
        m["xT_in"] = np.ascontiguousarray(np.concatenate([xs[b].T, xp[2 * b].T, xp[2 * b + 1].T], axis=1))
        cT = np.stack([fm(inp["c"][b]), fm(inp["c_ctx"])], axis=2).reshape(128, 16)
        m["condT"] = np.ascontiguousarray(cT)
        m["ckT"] = np.ascontiguousarray(np.asarray(inp["cache_attn_k"][b], np.float32).transpose(0, 1, 3, 2))
        m["cv"] = f(inp["cache_attn_v"][b])
        st = np.asarray(inp["state_rwkv"][b, 0], np.float32)
        m["rw_state"] = np.ascontiguousarray(st.transpose(0, 1, 3, 2).reshape(2, 8, 128, 64))
        maps.append(m)
    return maps


def assemble(results):
    y_prompt = np.zeros((16, 256, 1024), np.float32)
    y_sample = np.zeros((8, 1024, 1024), np.float32)
    new_k = np.zeros((16, 2, 8, 256, 128), np.float32)
    new_v = np.zeros((16, 2, 8, 256, 128), np.float32)
    new_s = np.zeros((16, 1, 2, 16, 64, 64), np.float32)
    for b in range(8):
        r = results[b]
        yT = r["yT"]
        y_sample[b] = yT[:, 0:1024].T
        y_prompt[2 * b] = yT[:, 1024:1280].T
        y_prompt[2 * b + 1] = yT[:, 1280:1536].T
        nk = r["newk"]
        nv = r["newv"]
        for p in range(2):
            new_k[2 * b + p] = nk[:, :, :, 256 * p:256 * (p + 1)].transpose(0, 1, 3, 2)
            new_v[2 * b + p] = nv[:, :, 256 * p:256 * (p + 1), :]
            ns = r["news"][p]
            new_s[2 * b + p, 0] = ns.reshape(2, 16, 64, 64).transpose(0, 1, 3, 2)
    return (y_prompt, y_sample, new_k, new_v, new_s)


_CACHE = {}


def kernel(**inputs):
    if "prog" not in _CACHE:
        _CACHE["prog"] = build_program()
    nc, c = _CACHE["prog"]
    in_maps = make_in_maps(inputs)
    res = run_bass_kernel_spmd(nc, in_maps, core_ids=list(range(8)))
    return assemble(res.results)
```

```python
from contextlib import ExitStack
import math
import numpy as np
import concourse.bass as bass
import concourse.mybir as mybir
from concourse.bass_utils import run_bass_kernel_spmd

F32 = mybir.dt.float32
BF16 = mybir.dt.bfloat16
AF = mybir.ActivationFunctionType
ALU = mybir.AluOpType
AX = mybir.AxisListType

ENGS = ("pe", "act", "dve", "pool", "sp")
SB_BASE = 16512
SB_END = 229376
SEM_CAP = 8000
SAME_ENG_SAFE_DIST = 10 ** 9
RELAX_SAME_ENG = True


class Buf:
    def __init__(self, t, name, space):
        self.t = t
        self.name = name
        self.space = space
        self.last_w = None
        self.readers = {}
        self.dma_readers = []
        self.dma_sem = None
        self.dma_count = 0

    def __getitem__(self, k):
        return self.t[k]


class Op:
    __slots__ = ("eng", "fn", "deps", "is_dma", "idx", "signal", "sig_val", "sem_owner", "dma_val", "tag")

    def __init__(self, eng, fn, is_dma=False, tag=""):
        self.eng = eng
        self.fn = fn
        self.deps = []
        self.is_dma = is_dma
        self.idx = -1
        self.signal = False
        self.sig_val = 0
        self.sem_owner = None
        self.dma_val = 0
        self.tag = tag


class Ctx:
    def __init__(self, nc):
        self.nc = nc
        self.stack = ExitStack()
        self.bufs = {}
        self.ops = {e: [] for e in ENGS}
        self.out_dma_ops = []
        self.n_waits = 0
        self.n_sems = 0
        self.sb_ptr = SB_BASE
        self.nview = 0

    def sbuf(self, name, shape, dtype=F32):
        esz = 2 if dtype == BF16 else 4
        nbytes = int(np.prod(shape[1:])) * esz
        nbytes = (nbytes + 31) // 32 * 32
        off = self.sb_ptr
        self.sb_ptr += nbytes
        assert self.sb_ptr <= SB_END, f"SBUF overflow at {name}: {self.sb_ptr}"
        t = self.nc.alloc_sbuf_tensor_at(name, list(shape), dtype, offset=off)
        b = Buf(t, name, "sbuf")
        b.off = off
        b.nbytes = nbytes
        self.bufs[t.name] = b
        return b

    def view(self, buf, name, shape, dtype=F32, byte_off=0):
        esz = 2 if dtype == BF16 else 4
        nbytes = int(np.prod(shape[1:])) * esz
        assert byte_off + nbytes <= buf.nbytes, (name, byte_off, nbytes, buf.nbytes)
        self.nview += 1
        t = self.nc.alloc_sbuf_tensor_at(f"{name}_v{self.nview}", list(shape), dtype, offset=buf.off + byte_off)
        self.bufs[t.name] = buf
        return t

    def psum(self, name, shape, dtype=F32):
        t = self.stack.enter_context(self.nc.psum_tensor(name, list(shape), dtype))
        b = Buf(t, name, "psum")
        self.bufs[name] = b
        return b

    def dram_in(self, name, shape, dtype=F32):
        return self.nc.dram_tensor(name, list(shape), dtype, kind="ExternalInput")

    def dram_out(self, name, shape, dtype=F32):
        return self.nc.dram_tensor(name, list(shape), dtype, kind="ExternalOutput")

    def dram_tmp(self, name, shape, dtype=F32):
        t = self.nc.dram_tensor(name, list(shape), dtype, kind="Internal")
        b = Buf(t, name, "dram")
        self.bufs[name] = b
        return b

    def _buf_of(self, ap):
        t = getattr(ap, "tensor", None)
        if t is None:
            return None
        return self.bufs.get(t.name)

    def _add(self, op, reads, writes):
        eng = op.eng
        lst = self.ops[eng]
        op.idx = len(lst)
        deps = []
        raw = set()
        for b in reads:
            if b.last_w is not None:
                deps.append(b.last_w)
                raw.add(id(b.last_w))
            if b.space == "psum":
                for e2, r in b.readers.items():
                    if e2 != eng:
                        deps.append(r)
        for b in writes:
            if b.last_w is not None:
                deps.append(b.last_w)
            deps.extend(b.readers.values())
            deps.extend(b.dma_readers)
        seen = set()
        for d in deps:
            if d is op or id(d) in seen:
                continue
            seen.add(id(d))
            if (not d.is_dma) and d.eng == eng and not op.is_dma:
                if eng == "pe":
                    continue
                if op.idx - d.idx >= SAME_ENG_SAFE_DIST:
                    continue
                if RELAX_SAME_ENG and id(d) not in raw:
                    continue
            op.deps.append(d)
        for b in reads:
            if op.is_dma:
                b.dma_readers.append(op)
            else:
                b.readers[eng] = op
        for b in writes:
            b.last_w = op
            b.readers = {}
            b.dma_readers = []
        lst.append(op)
        return op

    def op(self, eng, method, *args, **kwargs):
        rb, wb = list(kwargs.pop("_xr", [])), []
        items = list(enumerate(args)) + list(kwargs.items())
        for k, v in items:
            if not hasattr(v, "tensor"):
                continue
            b = self._buf_of(v)
            if b is None:
                continue
            if k in ("out", "accum_out") or k == 0:
                wb.append(b)
            else:
                rb.append(b)

        def fn(e, method=method, args=args, kwargs=kwargs):
            return getattr(e, method)(*args, **kwargs)

        return self._add(Op(eng, fn, tag=method), rb, wb)

    def dma(self, eng, out, in_, **kwargs):
        ob = self._buf_of(out)
        ib = self._buf_of(in_)
        if ob is not None and ob.space != "dram":
            owner = ob
        elif ib is not None and ib.space != "dram":
            owner = ib
        else:
            owner = ob if ob is not None else ib
        assert owner is not None

        def fn(e, out=out, in_=in_, kwargs=kwargs):
            return e.dma_start(out=out, in_=in_, **kwargs)

        o = Op(eng, fn, is_dma=True, tag="dma")
        o.sem_owner = owner
        owner.dma_count += 16
        o.dma_val = owner.dma_count
        self._add(o, [ib] if ib is not None else [], [ob] if ob is not None else [])
        if ob is None:
            self.out_dma_ops.append(o)
        return o

    def emit(self):
        nc = self.nc
        for e in ENGS:
            for o in self.ops[e]:
                for d in o.deps:
                    if not d.is_dma:
                        d.signal = True
        nsem = {}
        for e in ENGS:
            cnt = 0
            for o in self.ops[e]:
                if (not o.is_dma) and o.signal:
                    cnt += 1
                    o.sig_val = cnt
            nsem[e] = (cnt + SEM_CAP - 1) // SEM_CAP
        sems = {}
        for e in ENGS:
            sems[e] = [self.stack.enter_context(nc.semaphore(f"s_{e}_{i}")) for i in range(nsem[e])]
        ub = {id(b): b for b in self.bufs.values()}
        for b in ub.values():
            if b.dma_count > 0:
                b.dma_sem = self.stack.enter_context(nc.semaphore(f"d_{b.name}"))
        self.n_sems = sum(len(v) for v in sems.values()) + sum(1 for b in ub.values() if b.dma_sem is not None)

        def emit_engine(eng_name, e):
            waited = {}
            maxsem = {}
            for o in self.ops[eng_name]:
                need = {}
                for d in o.deps:
                    if d.is_dma:
                        key = ("dma", d.sem_owner.name)
                        val = d.dma_val
                        sem = d.sem_owner.dma_sem
                    else:
                        si = (d.sig_val - 1) // SEM_CAP
                        if maxsem.get(d.eng, -1) > si:
                            continue
                        key = (d.eng, si)
                        val = (d.sig_val - 1) % SEM_CAP + 1
                        sem = sems[d.eng][si]
                    if waited.get(key, 0) >= val:
                        continue
                    if key not in need or need[key][1] < val:
                        need[key] = (sem, val)
                if DBG.get("dump"):
                    DUMP.append((eng_name, o.idx, o.tag, o.sig_val if o.signal else 0, o.dma_val if o.is_dma else 0,
                                 o.sem_owner.name if o.is_dma else "", sorted((str(k), v[1]) for k, v in need.items())))
                for key, (sem, val) in need.items():
                    e.wait_ge(sem, val)
                    waited[key] = val
                    if key[0] != "dma":
                        maxsem[key[0]] = max(maxsem.get(key[0], -1), key[1])
                    self.n_waits += 1
                ins = o.fn(e)
                if o.is_dma:
                    ins.then_inc(o.sem_owner.dma_sem, 16)
                elif o.signal:
                    si = (o.sig_val - 1) // SEM_CAP
                    ins.then_inc(sems[eng_name][si], 1)
            if eng_name == "sp":
                fin = {}
                for o in self.out_dma_ops:
                    key = o.sem_owner.name
                    if key not in fin or fin[key][1] < o.dma_val:
                        fin[key] = (o.sem_owner.dma_sem, o.dma_val)
                for key, (sem, val) in fin.items():
                    e.wait_ge(sem, val)

        with nc.Block() as block:
            @block.tensor
            def _(e):
                emit_engine("pe", e)

            @block.scalar
            def _(e):
                emit_engine("act", e)

            @block.vector
            def _(e):
                emit_engine("dve", e)

            @block.gpsimd
            def _(e):
                emit_engine("pool", e)

            @block.sync
            def _(e):
                emit_engine("sp", e)


class Rot:
    def __init__(self, bufs):
        self.bufs = bufs
        self.i = 0

    def next(self):
        b = self.bufs[self.i % len(self.bufs)]
        self.i += 1
        return b


D = 1024
KC = 8
NT = 1536
DEPTH = 4
SEGS = [(0, 1024), (1024, 256), (1280, 256)]
TCH = [(0, 512, 0), (512, 512, 0), (1024, 512, 1)]
NORM_EPS = 1e-6
CONVW = 31
PADW = 15
POFF = [0, 1024 + 30, 1024 + 30 + 256 + 30]
PTOT = 1024 + 256 + 256 + 90


def vec_layout():
    off = {}
    n = 0

    def add(name, cols):
        nonlocal n
        off[name] = n
        n += cols

    for i in range(DEPTH):
        add(f"norm_w{i}", 8)
        add(f"ada_b{i}", 24)
    add("final_w", 8)
    add("subln0", 1)
    add("subln1", 1)
    for k in range(6):
        add(f"mu{k}", 8)
    for z in range(2):
        add(f"w0_{z}", 8)
        add(f"a0_{z}", 8)
    for nm in ("k_k", "k_a", "r_k", "rln_w", "rln_b", "cdw_b", "cln_w", "cln_b"):
        add(nm, 8)
    add("cdw_w", 8 * CONVW)
    return off, n


VOFF, NVEC = vec_layout()


def fm(v):
    return np.ascontiguousarray(np.asarray(v, np.float32).reshape(8, 128).T)


DBG = {}
DUMP = []


def build_program(n_layers=4, debug_x=False):
    nc = bass.Bass("TRN2", target_bir_lowering=False)
    c = Ctx(nc)

    d_xT = c.dram_in("xT_in", [D, NT]).ap()
    d_cond = c.dram_in("condT", [128, 16]).ap()
    d_vecs = c.dram_in("vecs", [128, NVEC]).ap()
    d_lam = c.dram_in("lam_bc", [128, 512]).ap()
    d_consts = c.dram_in("consts", [128, 6 * 128 + 64]).ap()
    d_cos = c.dram_in("cosT", [128, 1024]).ap()
    d_sin = c.dram_in("sinT", [128, 1024]).ap()
    d_ada_w = c.dram_in("ada_w", [4, D, 3 * D]).ap()
    d_attn_w_in = c.dram_in("attn_w_in", [2, D, 4 * D]).ap()
    d_attn_w_out = c.dram_in("attn_w_out", [2, D, D]).ap()
    d_ckT = c.dram_in("ckT", [2, 8, 128, 256]).ap()
    d_cv = c.dram_in("cv", [2, 8, 256, 128]).ap()
    d_conv_w_in = c.dram_in("conv_w_in", [1, D, 3 * D]).ap()
    d_conv_w_out = c.dram_in("conv_w_out", [1, D, D]).ap()
    d_rwkv_w_in = c.dram_in("rwkv_w_in", [1, D, 4 * D]).ap()
    d_rwkv_w_out = c.dram_in("rwkv_w_out", [1, D, D]).ap()
    d_w1cat = c.dram_in("rw_w1cat", [D, 128]).ap()
    d_a1cat = c.dram_in("rw_a1cat", [D, 128]).ap()
    d_w2cat = c.dram_in("rw_w2cat", [128, D]).ap()
    d_a2cat = c.dram_in("rw_a2cat", [128, D]).ap()
    d_state = c.dram_in("rw_state", [2, 8, 128, 64]).ap()

    o_yT = c.dram_out("yT", [D, NT]).ap()
    o_newk = c.dram_out("newk", [2, 8, 128, 512]).ap()
    o_newv = c.dram_out("newv", [2, 8, 512, 128]).ap()
    o_news = c.dram_out("news", [2, 2, 8, 128, 64]).ap()

    xT = [c.sbuf(f"xT{k}", [128, NT]) for k in range(KC)]
    hT = [c.sbuf(f"hT{k}", [128, NT], BF16) for k in range(KC)]
    ygT = [c.sbuf(f"ygT{k}", [128, NT], BF16) for k in range(KC)]
    S = [c.sbuf(f"S{k}", [128, 1632]) for k in range(DBG.get("nslab", 9))]
    if DBG.get("real_bf"):
        BfR = [c.sbuf(f"BfR{k}", [128, 1792], BF16) for k in range(4)]
    wA = [c.sbuf(f"wA{k}", [128, 8, 256], BF16) for k in range(2)]
    wI = [c.sbuf(f"wI{k}", [128, 8, 512], BF16) for k in range(2)]
    vecs = c.sbuf("vecs_sb", [128, NVEC])
    cond = c.sbuf("cond_sb", [128, 16])
    scT = c.sbuf("scT", [128, 16], BF16)
    consts = c.sbuf("consts_sb", [128, 3 * 128])
    cbf = c.sbuf("consts_bf", [128, 6 * 128 + 64], BF16)
    mod = c.sbuf("mod", [128, 48])
    sc1 = c.sbuf("sc1", [128, 16])
    small = c.sbuf("small", [128, 16])
    ECb = c.sbuf("ECb", [128, 16])
    HFB = [c.sbuf(f"Hfb{q_}", [128, 64]) for q_ in range(2)]
    PTB = [c.sbuf(f"PTb{q_}", [128, 4, 64], BF16) for q_ in range(2)]
    SGR1 = c.sbuf("sgr1", [128, 512], BF16)
    BON1 = c.sbuf("bonus1", [128, 512], BF16)
    HBF0 = c.sbuf("hbf0", [128, 64], BF16)
    E = [c.sbuf(f"E{k}", [128, 512], BF16) for k in range(4)]
    T5 = [c.sbuf(f"T5{k}", [128, 512]) for k in range(6)]
    RSTD = [c.sbuf(f"rstd{k}", [128, 512]) for k in range(2)]
    rotR = Rot(RSTD)

    def bview(buf, name, shape=(128, 1792)):
        return c.view(buf, name, list(shape), BF16)[:]

    pb = [c.psum(f"pb{k}", [128, 512]) for k in range(8)]
    rotS = Rot(pb[0:4])
    rotE = Rot(E)
    rotT = Rot(T5)

    ident_f = consts[:, 0:128]
    ones_f = consts[:, 128:256]
    ident_b = cbf[:, 0:128]
    ones_b = cbf[:, 128:256]
    PR_b = cbf[:, 256:384]

    def vcol(name, j=0, n=1):
        o = VOFF[name] + j
        return vecs[:, o:o + n]

    def MM(ps, lhsT, rhs, start, stop, xr=()):
        c.op("pe", "matmul", ps, lhsT=lhsT, rhs=rhs, start=start, stop=stop, _xr=list(xr))

    def ACT(out, in_, func, **kw):
        c.op("act", "activation", out=out, in_=in_, func=func, **kw)

    def TT(out, in0, in1, op, eng="dve"):
        c.op(eng, "tensor_tensor", out=out, in0=in0, in1=in1, op=op)

    def STT(out, in0, scalar, in1, op0, op1, eng="dve"):
        c.op(eng, "scalar_tensor_tensor", out=out, in0=in0, scalar=scalar, in1=in1, op0=op0, op1=op1)

    def TS(out, in0, s1, s2, op0, op1=None, eng="dve"):
        if op1 is None:
            c.op(eng, "tensor_scalar", out=out, in0=in0, scalar1=s1, scalar2=None, op0=op0)
        else:
            c.op(eng, "tensor_scalar", out=out, in0=in0, scalar1=s1, scalar2=s2, op0=op0, op1=op1)

    def CP(out, in_, eng="dve"):
        c.op(eng, "tensor_copy", out=out, in_=in_)

    def RCP(out, in_):
        c.op("dve", "reciprocal", out=out, in_=in_)

    def wview(dram_w):
        return dram_w.rearrange("(k p) n -> p k n", p=128)

    c.dma("sp", out=vecs[:], in_=d_vecs)
    c.dma("sp", out=cond[:], in_=d_cond)
    c.dma("sp", out=consts[:, 0:256], in_=d_consts[:, 0:256])
    c.dma("sp", out=consts[:, 256:384], in_=d_consts[:, 640:768])
    c.dma("pool", out=cbf[:], in_=d_consts)
    for k in range(KC):
        c.dma("sp", out=xT[k][:], in_=d_xT[k * 128:(k + 1) * 128, :])
    ACT(scT[:], cond[:], AF.Silu)

    mods = [mod, c.sbuf("mod_b", [128, 48])]
    sc1s = [sc1, c.sbuf("sc1_b", [128, 16])]
    gts = [c.sbuf("gate_a", [128, 16]), c.sbuf("gate_b", [128, 16])]
    cur = {"i": 0, "ada": None}

    def adaln_steps(i):
        psada = pb[7]
        md, s1 = mods[i % 2], sc1s[i % 2]
        wv = wview(d_ada_w[i])
        for bi in range(12):
            slot = wA[bi % 2]
            c.dma("pool", out=slot[:], in_=wv[:, :, bi * 256:(bi + 1) * 256])
            for m2 in range(2):
                m = bi * 2 + m2
                for k in range(KC):
                    MM(psada[:, 2 * m:2 * m + 2], slot[:, k, m2 * 128:(m2 + 1) * 128], scT[:, 2 * k:2 * k + 2],
                       start=(k == 0), stop=(k == KC - 1))
            if bi == 7:
                ab = vcol(f"ada_b{i}", 0, 16)
                TT(md[:, 0:32].rearrange("p (m t) -> p m t", t=2), psada[:, 0:32].rearrange("p (m t) -> p m t", t=2),
                   ab.unsqueeze(2).to_broadcast([128, 16, 2]), ALU.add)
                nw = vcol(f"norm_w{i}", 0, 8)
                STT(s1[:].rearrange("p (m t) -> p m t", t=2), md[:, 16:32].rearrange("p (m t) -> p m t", t=2), 1.0,
                    nw.unsqueeze(2).to_broadcast([128, 8, 2]), ALU.add, ALU.mult)
            yield
        ab = vcol(f"ada_b{i}", 16, 8)
        TT(gts[i % 2][:].rearrange("p (m t) -> p m t", t=2), psada[:, 32:48].rearrange("p (m t) -> p m t", t=2),
           ab.unsqueeze(2).to_broadcast([128, 8, 2]), ALU.add)
        yield

    def ada_advance(n):
        g = cur["ada"]
        if g is None:
            return
        for _ in range(n):
            try:
                next(g)
            except StopIteration:
                cur["ada"] = None
                return

    def shift_ap(k, ci):
        return mods[cur["i"] % 2][:, 2 * k + ci:2 * k + ci + 1]

    def sc1_ap(k, ci):
        return sc1s[cur["i"] % 2][:, 2 * k + ci:2 * k + ci + 1]

    def gate_ap(k, ci):
        return gts[cur["i"] % 2][:, 2 * k + ci:2 * k + ci + 1]

    def rms_rstd(tc, eps, src, nchunks=KC, scale=1.0 / D):
        t0, tl, ci = tc
        ps = rotS.next()
        for k in range(nchunks):
            sq = rotT.next()
            TT(sq[:, :tl], src[k][:, t0:t0 + tl], src[k][:, t0:t0 + tl], ALU.mult)
            MM(ps[:, :tl], ones_f, sq[:, :tl], start=(k == 0), stop=(k == nchunks - 1))
        sd = rotT.next()
        ACT(sd[:, :tl], ps[:, :tl], AF.Ln, bias=float(eps), scale=float(scale))
        rstd = rotR.next()
        ACT(rstd[:, :tl], sd[:, :tl], AF.Exp, scale=-0.5)
        return rstd

    def norm_mod(i):
        for tc in TCH:
            t0, tl, ci = tc
            rstd = rms_rstd(tc, NORM_EPS, xT)
            for k in range(KC):
                t = rotT.next()
                TT(t[:, :tl], xT[k][:, t0:t0 + tl], rstd[:, :tl], ALU.mult)
                ACT(hT[k][:, t0:t0 + tl], t[:, :tl], AF.Identity, scale=sc1_ap(k, ci), bias=shift_ap(k, ci))

    def out_proj(d_wout):
        wv = wview(d_wout)
        for n2 in range(2):
            c.dma("pool", out=wI[n2][:], in_=wv[:, :, n2 * 512:(n2 + 1) * 512])
        for m in range(KC):
            slot = wI[m // 4]
            for tc in TCH:
                t0, tl, ci = tc
                ps = rotS.next()
                for k in range(KC):
                    MM(ps[:, :tl], slot[:, k, (m % 4) * 128:(m % 4 + 1) * 128], ygT[k][:, t0:t0 + tl],
                       start=(k == 0), stop=(k == KC - 1))
                STT(xT[m][:, t0:t0 + tl], ps[:, :tl], gate_ap(m, ci), xT[m][:, t0:t0 + tl], ALU.mult, ALU.add)
            ada_advance(2)
        ada_advance(100)

    def proj_fm(ps, wslot, sel, tc):
        t0, tl, ci = tc
        for k in range(KC):
            MM(ps[:, :tl], wslot[:, k, sel], hT[k][:, t0:t0 + tl], start=(k == 0), stop=(k == KC - 1))

    def attn_layer(i, j):
        lam_init = 0.8 - 0.6 * math.exp(-0.3 * i)
        cosT, sinT = S[0], S[1]
        c.dma("sp", out=cosT[:, 0:1024], in_=d_cos)
        c.dma("sp", out=sinT[:, 0:1024], in_=d_sin)
        lam_sb = rotT.next()
        c.dma("sp", out=lam_sb[:], in_=d_lam)
        lo = j * 256
        prod = rotT.next()
        TT(prod[:, 0:64], lam_sb[:, lo:lo + 64], lam_sb[:, lo + 64:lo + 128], ALU.mult)
        TT(prod[:, 64:128], lam_sb[:, lo + 128:lo + 192], lam_sb[:, lo + 192:lo + 256], ALU.mult)
        c.op("dve", "reduce_sum", out=small[:, 0:1], in_=prod[:, 0:64], axis=AX.X)
        c.op("dve", "reduce_sum", out=small[:, 1:2], in_=prod[:, 64:128], axis=AX.X)
        ACT(small[:, 2:4], small[:, 0:2], AF.Exp)
        STT(small[:, 4:5], small[:, 3:4], -lam_init, small[:, 2:3], ALU.add, ALU.subtract)
        neglam = small[:, 4:5]
        TS(small[:, 5:6], vcol(f"subln{j}"), 1.0 - lam_init, None, ALU.mult)
        subw = small[:, 5:6]

        if DBG.get("real_bf"):
            q_bf, k_bf, sg, vkc = [b.t[:] for b in BfR]
        else:
            q_bf, k_bf, sg, vkc = bview(S[2], "qbf"), bview(S[3], "kbf"), bview(S[4], "sgb"), bview(S[5], "vkc")
        if DBG.get("stop") == "attn_setup":
            return
        wv = d_attn_w_in[j].rearrange("(k p) (n c) -> p k n c", p=128, c=1024)
        for h in range(DBG.get("nheads", 8)):
            slot = wI[h % 2]
            sl4 = slot[:].rearrange("p k (n c) -> p k n c", c=128)
            for n4 in range(4):
                c.dma("pool", out=sl4[:, :, n4, :], in_=wv[:, :, n4, h * 128:(h + 1) * 128])
            if not DBG.get("no_cache"):
                c.dma("pool", out=k_bf[:, 1536:1792], in_=d_ckT[j, h])
                c.dma("pool", out=vkc[:, 1536:1792].rearrange("p (t d) -> p t d", d=128),
                      in_=d_cv[j, h].rearrange("(t p) d -> p t d", p=128))
            items = [(which, dst, tc) for which, dst in ((0, q_bf), (1, k_bf)) for tc in TCH]

            def qk_proj(it):
                which, dst, tc = it
                ps = rotS.next()
                proj_fm(ps, slot, slice(which * 128, (which + 1) * 128), tc)
                return ps

            def qk_post(it, ps):
                which, dst, tc = it
                t0, tl, ci = tc
                if ci == 0 and not DBG.get("no_rope"):
                    a_bf = rotE.next()
                    ACT(a_bf[:, :tl], ps[:, :tl], AF.Copy)
                    ps2 = rotS.next()
                    MM(ps2[:, :tl], PR_b, a_bf[:, :tl], start=True, stop=True)
                    t1 = rotT.next()
                    t2 = rotT.next()
                    TT(t1[:, :tl], ps[:, :tl], cosT[:, t0:t0 + tl], ALU.mult)
                    TT(t2[:, :tl], ps2[:, :tl], sinT[:, t0:t0 + tl], ALU.mult)
                    TT(dst[:, t0:t0 + tl], t1[:, :tl], t2[:, :tl], ALU.add)
                else:
                    ACT(dst[:, t0:t0 + tl], ps[:, :tl], AF.Copy)
                    if which == 1 and ci == 1 and not DBG.get("no_out"):
                        kst = rotT.next()
                        CP(kst[:, :tl], ps[:, :tl])
                        c.dma("sp", out=o_newk[j, h], in_=kst[:, :tl])

            ps_cur = qk_proj(items[0])
            for qi in range(len(items)):
                ps_nxt = qk_proj(items[qi + 1]) if qi + 1 < len(items) else None
                qk_post(items[qi], ps_cur)
                ps_cur = ps_nxt
            for t4 in range(3):
                ps = rotS.next()
                for tt in range(4):
                    tok0 = (t4 * 4 + tt) * 128
                    for k in range(KC):
                        MM(ps[:, tt * 128:(tt + 1) * 128], hT[k][:, tok0:tok0 + 128], slot[:, k, 256:384],
                           start=(k == 0), stop=(k == KC - 1))
                CP(vkc[:, t4 * 512:(t4 + 1) * 512], ps[:, :])
                if t4 == 2 and not DBG.get("no_out"):
                    vst = rotT.next()
                    ACT(vst[:, :], ps[:, :], AF.Copy)
                    c.dma("sp", out=o_newv[j, h].rearrange("(t p) d -> p t d", p=128),
                          in_=vst[:, :].rearrange("p (t d) -> p t d", d=128))
            for tc in TCH:
                t0, tl, ci = tc
                ps = rotS.next()
                proj_fm(ps, slot, slice(384, 512), tc)
                ACT(sg[:, t0:t0 + tl], ps[:, :tl], AF.Silu)
            if DBG.get("stop") == "attn_proj":
                continue
            jobs = []
            own = [(k_bf, kt * 128, kt) for kt in range(8)] + [(k_bf, 1536, 12), (k_bf, 1664, 13)]
            jobs.append((0, 512, [e_ + (0, 512) for e_ in own]))
            jobs.append((512, 512, [e_ + (512, 512) for e_ in own]))
            kl = []
            for p in range(2):
                kl += [(k_bf, 1024 + 256 * p + 128 * t, 8 + 2 * p + t, 1024 + 256 * p, 256) for t in range(2)]
            jobs.append((1024, 512, kl))
            for (q0, qn, klist) in jobs:
                pO1, pO2, pZ1, pZ2 = pb[4], pb[5], pb[6], pb[7]
                nk = len(klist)

                def scores(ki):
                    ksrc, kcol, vt, qa, qw = klist[ki]
                    pS1 = rotS.next()
                    pS2 = rotS.next()
                    MM(pS1[:, :qw], ksrc[0:64, kcol:kcol + 128], q_bf[0:64, qa:qa + qw], start=True, stop=True)
                    MM(pS2[:, :qw], ksrc[64:128, kcol:kcol + 128], q_bf[64:128, qa:qa + qw], start=True, stop=True)
                    e1 = rotE.next()
                    e2 = rotE.next()
                    ACT(e1[:, :qw], pS1[:, :qw], AF.Exp, scale=0.125)
                    ACT(e2[:, :qw], pS2[:, :qw], AF.Exp, scale=0.125)
                    return e1, e2

                nxt = scores(0)
                for ki in range(nk):
                    e1, e2 = nxt
                    if ki + 1 < nk:
                        nxt = scores(ki + 1)
                    _, _, vt, qa, qw = klist[ki]
                    vt_ap = vkc[:, vt * 128:(vt + 1) * 128]
                    same = [i_ for i_ in range(nk) if klist[i_][3] == qa]
                    st, sp_ = (ki == same[0]), (ki == same[-1])
                    cs = slice(qa - q0, qa - q0 + qw)
                    MM(pO1[:, cs], vt_ap, e1[:, :qw], start=st, stop=sp_)
                    MM(pZ1[:, cs], ones_b, e1[:, :qw], start=st, stop=sp_)
                    MM(pO2[:, cs], vt_ap, e2[:, :qw], start=st, stop=sp_)
                    MM(pZ2[:, cs], ones_b, e2[:, :qw], start=st, stop=sp_)
                rz1 = rotT.next()
                rz2 = rotT.next()
                oc1 = rotT.next()
                oc2 = rotT.next()
                CP(oc1[:, :qn], pO1[:, :qn])
                ACT(rz1[:, :qn], pZ1[:, :qn], AF.Ln)
                CP(oc2[:, :qn], pO2[:, :qn])
                ACT(rz2[:, :qn], pZ2[:, :qn], AF.Ln)
                ACT(rz1[:, :qn], rz1[:, :qn], AF.Exp, scale=-1.0)
                ACT(rz2[:, :qn], rz2[:, :qn], AF.Exp, scale=-1.0)
                TT(oc1[:, :qn], oc1[:, :qn], rz1[:, :qn], ALU.mult)
                TT(oc2[:, :qn], oc2[:, :qn], rz2[:, :qn], ALU.mult)
                o = rotT.next()
                STT(o[:, :qn], oc2[:, :qn], neglam, oc1[:, :qn], ALU.mult, ALU.add)
                sq = rotT.next()
                TT(sq[:, :qn], o[:, :qn], o[:, :qn], ALU.mult)
                psq = rotS.next()
                MM(psq[:, :qn], ones_f, sq[:, :qn], start=True, stop=True)
                rs = rz1
                ACT(rs[:, :qn], psq[:, :qn], AF.Ln, bias=1e-5, scale=1.0 / 128)
                ACT(rs[:, :qn], rs[:, :qn], AF.Exp, scale=-0.5)
                on = oc1
                TT(on[:, :qn], o[:, :qn], rs[:, :qn], ALU.mult)
                STT(ygT[h][:, q0:q0 + qn], on[:, :qn], subw, sg[:, q0:q0 + qn], ALU.mult, ALU.mult)
        out_proj(d_attn_w_out[j])


    LSC = math.exp(-0.5)

    def rwkv_layer(i, j):
        maskA_b = cbf[:, 384:512]
        maskB_b = cbf[:, 512:640]
        blk_f = consts[:, 256:384]
        xsp = [c.dram_tmp(f"xspill{k}", [128, NT]) for k in range(KC)]
        for k in range(KC):
            c.dma("sp", out=xsp[k].t.ap(), in_=xT[k][:])
        V_ = c.view
        dxT = [V_(S[k // 2], f"dxT{k}", [128, NT], BF16, (k % 2) * 3072) for k in range(KC)]
        t1T = V_(S[4], "t1T", [128, NT], BF16, 0)
        la1T = V_(S[4], "la1T", [128, NT], BF16, 3072)
        w2c = V_(S[5], "w2c", [128, 1024], BF16, 0)
        a2c = V_(S[5], "a2c", [128, 1024], BF16, 2048)
        omka = V_(S[5], "omka", [128, 8], F32, 4096)
        dxtmp = S[6]
        splits = {}

        def VS(parent, name, shape, dtype, off):
            esz = 2 if dtype == BF16 else 4
            nb = int(np.prod(shape[1:])) * esz
            assert off + nb <= parent.nbytes
            for (_, o2, n2) in splits.get(parent.name, []):
                assert off + nb <= o2 or o2 + n2 <= off, "split views must be disjoint"
            c.nview += 1
            t_ = nc.alloc_sbuf_tensor_at(f"{name}_s{c.nview}", list(shape), dtype, offset=parent.off + off)
            b_ = Buf(t_, t_.name, "sbuf")
            b_.off, b_.nbytes = parent.off + off, nb
            c.bufs[t_.name] = b_
            splits.setdefault(parent.name, []).append(((parent, b_, t_), off, nb))
            return t_

        kT = VS(xT[0], "kT", [128, 1024], F32, 0)
        rT = VS(xT[0], "rT", [128, 1024], BF16, 4096)
        ld = VS(xT[1], "ld", [128, 1024], F32, 0)
        vT = VS(xT[1], "vT", [128, 1024], BF16, 4096)
        Lcp = VS(S[6], "Lcp", [128, 1032], F32, 0)
        sgr = VS(S[6], "sgr", [128, 1024], BF16, 4128)
        yacc = VS(xT[2], "yacc", [128, 1024], F32, 0)
        kk = VS(xT[2], "kk", [128, 1024], BF16, 4096)
        kd = [VS(xT[3], f"kd{z}", [128, 1024], BF16, 2048 * z) for z in range(2)]
        bonus = VS(xT[3], "bonus", [128, 1024], BF16, 4096)
        kka = [VS(xT[4], f"kka{z}", [128, 1024], BF16, 2048 * z) for z in range(2)]
        vs = VS(xT[4], "vs", [128, 1024], BF16, 4096)

        def handover(to_children):
            for pname, lst in splits.items():
                parent = lst[0][0][0]
                if to_children:
                    c.op("dve", "memset", parent[:, 0:1], 0.0)
                    for ((_, b_, t_), _, _) in lst:
                        c.op("dve", "memset", t_[:, 0:1], 0.0, _xr=[parent])
                else:
                    for ((_, b_, t_), _, _) in lst:
                        c.op("dve", "memset", t_[:, 0:1], 0.0)
                    c.op("dve", "memset", parent[:, 0:1], 0.0, _xr=[l[0][1] for l in lst])
        KR = V_(xT[6], "KR", [128, 16, 2, 64], BF16, 0)
        BK = V_(xT[7], "BK", [128, 16, 2, 64], BF16, 0)
        BKh = V_(S[7], "BKh", [128, 16, 2, 64], BF16, 0)
        MAB4, TOK, S6 = [], [], []
        for p, b1 in enumerate((S[8], xT[5], wA[0], wA[1])):
            MAB4.append(V_(b1, f"MAB4{p}", [128, 4, 256], BF16, 0))
            TOK.append(V_(b1, f"TOK{p}", [128, 4, 192], BF16, 2048))
            S6.append(V_(b1, f"S6{p}", [128, 4, 64], BF16, 3584))
        PSs = [V_(RSTD[0], "PS0", [128, 4, 128], BF16, 0), V_(E[0], "PS1", [128, 4, 128], BF16, 0)]
        PSS, PSSB = [], []
        for q_, base_ in enumerate((RSTD[0], E[0])):
            t_ = nc.alloc_sbuf_tensor_at(f"PSS{q_}", [128, 4, 128], BF16, offset=base_.off)
            b_ = Buf(t_, f"PSS{q_}", "sbuf")
            b_.off, b_.nbytes = base_.off, 1024
            c.bufs[t_.name] = b_
            PSS.append(t_)
            PSSB.append(b_)
            c.op("dve", "memset", base_[:, 32:33], 0.0)
            c.op("dve", "memset", t_[:, 0, 64:65], 0.0, _xr=[base_])
        PTs = [PTB[0].t, PTB[1].t]
        SMALL = []
        for q_, (smz, smu, smh) in enumerate(((E[3], E[2], HBF0), (E[0], E[1], RSTD[1]))):
            SMALL.append((V_(smz, f"Zs{q_}", [128, 64], BF16, 0), V_(smu, f"Us{q_}", [128, 64], BF16, 0),
                          V_(smh, f"Hbf{q_}", [128, 64], BF16, 0), HFB[q_]))
        EC = ECb
        ident2_b = cbf[:, 768:832]

        dxtmps = [S[6], S[7]]
        for k in range(KC):
            dxtmp = dxtmps[k % 2]
            en = "pool" if k % 4 == 3 else "dve"
            TT(dxtmp[:, 1:NT - 1], hT[k][:, 0:NT - 2], hT[k][:, 2:NT], ALU.add, eng=en)
            for (s0, L) in SEGS:
                ACT(dxtmp[:, s0:s0 + 1], hT[k][:, s0 + 1:s0 + 2], AF.Copy)
                ACT(dxtmp[:, s0 + L - 1:s0 + L], hT[k][:, s0 + L - 2:s0 + L - 1], AF.Copy)
            STT(dxT[k][:, :], dxtmp[:, 0:NT], 0.5, hT[k][:, :], ALU.mult, ALU.subtract)
        c.dma("pool", out=w2c[:, :], in_=d_w2cat)
        c.dma("pool", out=a2c[:, :], in_=d_a2cat)
        TS(omka[:, :], vcol("k_a", 0, 8), -1.0, 1.0, ALU.mult, ALU.add)
        for li, (dsrc, mun, dst, func) in enumerate(((d_w1cat, "mu4", t1T, AF.Tanh), (d_a1cat, "mu5", la1T, AF.Copy))):
            slot = wA[li]
            c.dma("pool", out=slot[:, :, 0:128], in_=dsrc.rearrange("(k p) n -> p k n", p=128))
            TT(slot[:, :, 128:256], slot[:, :, 0:128], vcol(mun, 0, 8).unsqueeze(2).to_broadcast([128, 8, 128]), ALU.mult)
            for tc in TCH:
                t0, tl, ci = tc
                ps = rotS.next()
                for k in range(KC):
                    MM(ps[:, :tl], slot[:, k, 0:128], hT[k][:, t0:t0 + tl], start=(k == 0), stop=False)
                for k in range(KC):
                    MM(ps[:, :tl], slot[:, k, 128:256], dxT[k][:, t0:t0 + tl], start=False, stop=(k == KC - 1))
                ACT(dst[:, t0:t0 + tl], ps[:, :tl], func)

        def run_interleaved(lists):
            n = max(len(l) for l in lists) if lists else 0
            for q in range(n):
                for l in lists:
                    if q < len(l):
                        l[q]()

        sgr1 = SGR1.t
        bonus1 = BON1.t
        wv = d_rwkv_w_in[j].rearrange("(k p) (n c) -> p k n c", p=128, c=1024)
        PASSES = [(0, 1024, [(0, 1024)], [(0, 512), (512, 512)], 0), (1024, 512, [(0, 256), (256, 256)], [(0, 512)], 1)]

        def scan_ap(arr, z, a0, b0, seg, rows=slice(0, 128)):
            s0, L = seg
            if z == 0:
                return arr[rows, a0:b0]
            lo = s0 + (s0 + L - b0)
            hi = s0 + (s0 + L - a0)
            return arr[rows, lo:hi][:, ::-1]

        W, Ws = wI[0], wI[1]
        W4 = W[:].rearrange("p k (n c) -> p k n c", c=128)
        Ws4 = Ws[:].rearrange("p k (n c) -> p k n c", c=128)

        def load_weights(hp_):
            for n4 in range(4):
                c.dma("pool", out=W4[:, :, n4, :], in_=wv[:, :, n4, hp_ * 128:(hp_ + 1) * 128])
                TT(Ws4[:, :, n4, :], W4[:, :, n4, :], vcol(f"mu{n4}", 0, 8).unsqueeze(2).to_broadcast([128, 8, 128]), ALU.mult)

        NHP = DBG.get("nhp", 8)
        handover(True)
        load_weights(0)
        for hp in range(NHP):
            def make_pass(tok0, TL, segs, tcs, ci, sgr, bonus):
                nch = TL // 64
                def base_proj1(lo, n, n4):
                    g0 = tok0 + lo
                    if True:
                        ps = rotS.next()
                        for k in range(KC):
                            MM(ps[:, :n], W[:, k, n4 * 128:(n4 + 1) * 128], hT[k][:, g0:g0 + n], start=(k == 0), stop=False)
                        for k in range(KC):
                            MM(ps[:, :n], Ws[:, k, n4 * 128:(n4 + 1) * 128], dxT[k][:, g0:g0 + n], start=False, stop=(k == KC - 1))
                        if n4 == 0:
                            ACT(rT[:, lo:lo + n], ps[:, :n], AF.Copy)
                        elif n4 == 1:
                            CP(kT[:, lo:lo + n], ps[:, :n])
                        elif n4 == 2:
                            ACT(vT[:, lo:lo + n], ps[:, :n], AF.Copy)
                        else:
                            ACT(sgr[:, lo:lo + n], ps[:, :n], AF.Silu)

                def base_chain(lo, n):
                    g0 = tok0 + lo
                    kkf, sq, mx = T5[0], T5[1], T5[2]
                    TS(kkf[:, :n], kT[:, lo:lo + n], vcol("k_k", hp), None, ALU.mult)
                    TT(sq[:, :n], kkf[:, :n], kkf[:, :n], ALU.mult)
                    ps = rotS.next()
                    MM(ps[:, :n], blk_f, sq[:, :n], start=True, stop=True)
                    TS(mx[:, :n], ps[:, :n], 1e-24, None, ALU.max)
                    ACT(mx[:, :n], mx[:, :n], AF.Ln)
                    ACT(mx[:, :n], mx[:, :n], AF.Exp, scale=-0.5)
                    TT(kk[:, lo:lo + n], kkf[:, :n], mx[:, :n], ALU.mult)
                    for z in range(2):
                        az, tmp = T5[3 + z], T5[5]
                        ps = rotS.next()
                        MM(ps[:, :n], a2c[z * 64:(z + 1) * 64, hp * 128:(hp + 1) * 128], la1T[z * 64:(z + 1) * 64, g0:g0 + n],
                           start=True, stop=True)
                        ACT(az[:, :n], ps[:, :n], AF.Sigmoid, bias=vcol(f"a0_{z}", hp))
                        TS(tmp[:, :n], az[:, :n], vcol("k_a", hp), omka[:, hp:hp + 1], ALU.mult, ALU.add)
                        TT(kd[z][:, lo:lo + n], tmp[:, :n], kT[:, lo:lo + n], ALU.mult)
                        TT(kka[z][:, lo:lo + n], kk[:, lo:lo + n], az[:, :n], ALU.mult)
                    tmp, rk = T5[0], T5[1]
                    TT(tmp[:, :n], kd[0][:, lo:lo + n], kd[1][:, lo:lo + n], ALU.add)
                    STT(rk[:, :n], rT[:, lo:lo + n], vcol("r_k", hp), tmp[:, :n], ALU.mult, ALU.mult)
                    ps = rotS.next()
                    MM(ps[:, :n], blk_f, rk[:, :n], start=True, stop=True)
                    TT(bonus[:, lo:lo + n], ps[:, :n], vT[:, lo:lo + n], ALU.mult)

                def base_proj(lo, n):
                    for n4 in range(4):
                        base_proj1(lo, n, n4)

                def base_all():
                    base_proj(*tcs[0])
                    for ti in range(len(tcs)):
                        if ti + 1 < len(tcs):
                            base_proj(*tcs[ti + 1])
                        base_chain(*tcs[ti])

                def base_steps():
                    st = []
                    for (lo_, n_) in tcs:
                        for n4 in range(4):
                            st.append(lambda lo_=lo_, n_=n_, n4=n4: base_proj1(lo_, n_, n4))
                        st.append(lambda lo_=lo_, n_=n_: base_chain(lo_, n_))
                    return st


                def scan_z(z):
                    for (lo, n) in tcs:
                        g0 = tok0 + lo
                        ps = rotS.next()
                        MM(ps[:, :n], w2c[z * 64:(z + 1) * 64, hp * 128:(hp + 1) * 128], t1T[z * 64:(z + 1) * 64, g0:g0 + n],
                           start=True, stop=True)
                        for seg in segs:
                            s0, L = seg
                            a0, b0 = max(lo, s0), min(lo + n, s0 + L)
                            if a0 >= b0:
                                continue
                            if z == 0:
                                dst = ld[:, a0:b0]
                            else:
                                dst = ld[:, s0 + (s0 + L - b0):s0 + (s0 + L - a0)][:, ::-1]
                            ACT(dst, ps[:, a0 - lo:b0 - lo], AF.Sigmoid, bias=vcol(f"w0_{z}", hp))
                    if z == 1:
                        for (s0, L) in segs:
                            CP(vs[:, s0:s0 + L], vT[:, s0:s0 + L][:, ::-1], eng="pool")
                    vsrc = vT if z == 0 else vs
                    c.op("dve", "memset", Lcp[:, 0:1], 0.0)
                    c.op("dve", "tensor_tensor_scan", out=Lcp[:, 1:TL + 1], data0=ones_f[:, 0:1].to_broadcast([128, TL]),
                         data1=ld[:, 0:TL], initial=0.0, op0=ALU.mult, op1=ALU.add)
                    for p0 in range(0, TL, 512):
                        Lloc, Ep, Em, Epv, Eh = T5[0], T5[1], T5[2], T5[3], T5[4]
                        v3 = lambda ap: ap.rearrange("p (c t) -> p c t", t=64)
                        TT(v3(Lloc[:, :]), v3(Lcp[:, 1 + p0:1 + p0 + 512]),
                           Lcp[:, p0:p0 + 512:64].unsqueeze(2).to_broadcast([128, 8, 64]), ALU.subtract)
                        ACT(Ep[:, :], Lloc[:, :], AF.Exp, scale=-LSC)
                        ACT(Em[:, :], Lloc[:, :], AF.Exp, scale=LSC)
                        TT(Epv[:, :], Lloc[:, :], ld[:, p0:p0 + 512], ALU.subtract)
                        ACT(Epv[:, :], Epv[:, :], AF.Exp, scale=-LSC)
                        TT(v3(Eh[:, :]), v3(Lloc[:, :])[:, :, 63:64].to_broadcast([128, 8, 64]), v3(Lloc[:, :]), ALU.subtract)
                        ACT(Eh[:, :], Eh[:, :], AF.Exp, scale=-LSC)
                        c0 = p0 // 64
                        ACT(EC[:, c0:c0 + 8], v3(Lloc[:, :])[:, :, 63], AF.Exp, scale=-LSC)
                        for seg in segs:
                            s0, L = seg
                            a0, b0 = max(p0, s0), min(p0 + 512, s0 + L)
                            if a0 >= b0:
                                continue
                            ca, cb = a0 // 64, b0 // 64
                            e = lambda buf: v3(buf[:, a0 - p0:b0 - p0])
                            sa = lambda arr: v3(scan_ap(arr, z, a0, b0, seg))
                            TT(KR[:, ca:cb, 0, :], sa(kk), e(Epv), ALU.mult)
                            TT(KR[:, ca:cb, 1, :], sa(rT), e(Ep), ALU.mult)
                            TT(BK[:, ca:cb, 0, :], sa(kka[z]), e(Em), ALU.mult)
                            TT(BK[:, ca:cb, 1, :], sa(kd[z]), e(Em), ALU.mult)
                            TT(BKh[:, ca:cb, 0, :], sa(kka[z]), e(Eh), ALU.mult)
                            TT(BKh[:, ca:cb, 1, :], sa(kd[z]), e(Eh), ALU.mult)

                    groups = []
                    for si, (s0, L) in enumerate(segs):
                        for gq in range(L // 256):
                            groups.append((si, s0, L, (s0 + gq * 256) // 64, gq == 0, gq == L // 256 - 1))
                    PAS = (slice(0, 64), slice(64, 128))

                    def MM2(bank, cols, lhs_fn, rhs_fn):
                        for a in range(2):
                            pa = PAS[a]
                            MM(bank[pa, cols], lhs_fn(pa, a), rhs_fn(pa, a), start=True, stop=True)

                    def yacc_ap(seg_s0, seg_L, p_lo, p_hi):
                        if z == 0:
                            return yacc[:, p_lo:p_hi]
                        return yacc[:, seg_s0 + (seg_s0 + seg_L - p_hi):seg_s0 + (seg_s0 + seg_L - p_lo)][:, ::-1]

                    def pre_stages(gi, strm):
                        si, s0, L, c0, first, last = groups[gi]
                        p = gi % 4
                        PS, PT = PSs[strm], PTs[strm]
                        PSs_, PSb_ = PSS[strm], PSSB[strm]
                        B0, B1 = pb[2 * strm], pb[2 * strm + 1]
                        st = []
                        for ii in range(4):
                            def sb(ii=ii):
                                cc = c0 + ii
                                bank = (B0, B1)[ii % 2]
                                krf = lambda pa, a: KR[pa, cc, :, :].rearrange("p a b -> p (a b)")
                                MM2(bank, slice(0, 128), lambda pa, a: BK[pa, cc, 0, :], krf)
                                MM2(bank, slice(128, 256), lambda pa, a: BK[pa, cc, 1, :], krf)
                                MM2(bank, slice(256, 320), lambda pa, a: KR[pa, cc, 0, :], lambda pa, a: BK[pa, cc, 0, :])
                                TT(MAB4[p][:, ii, :].rearrange("p (h t) -> p h t", t=128),
                                   bank[:, 0:256].rearrange("p (h t) -> p h t", t=128),
                                   maskA_b.unsqueeze(1).to_broadcast([128, 2, 128]), ALU.mult)
                                TT(PT[:, ii, :], bank[:, 256:320], maskB_b[:, 0:64], ALU.mult)
                            st.append(sb)

                        def sinit():
                            TT(PSs_[:, :, 64:128], ident2_b.unsqueeze(1).to_broadcast([128, 4, 64]), MAB4[p][:, :, 0:64], ALU.subtract)
                        st.append(sinit)
                        for lvl in range(6):
                            def sl(lvl=lvl):
                                bP, bT = B0, B1
                                bP3 = bP[:, :].rearrange("p (c t) -> p c t", t=128)
                                if lvl == 0:
                                    for ii in range(4):
                                        MM2(bP, slice(ii * 128, ii * 128 + 64), lambda pa, a: PT[pa, ii, :], lambda pa, a: MAB4[p][pa, ii, 0:64])
                                    for ii in range(4):
                                        MM2(bT, slice(ii * 64, (ii + 1) * 64), lambda pa, a: MAB4[p][pa, ii, 0:64], lambda pa, a: PT[pa, ii, :])
                                else:
                                    for ii in range(4):
                                        for a in range(2):
                                            pa = PAS[a]
                                            MM(bP[pa, ii * 128:(ii + 1) * 128], PT[pa, ii, :], PS[pa, ii, :], start=True, stop=True, xr=[PSb_])
                                    if lvl < 5:
                                        for ii in range(4):
                                            MM2(bT, slice(ii * 64, (ii + 1) * 64), lambda pa, a: PS[pa, ii, 0:64], lambda pa, a: PT[pa, ii, :])
                                    TT(PSs_[:, :, 64:128], PSs_[:, :, 64:128], bP3[:, :, 64:128], ALU.add)
                                if lvl < 5:
                                    ACT(PS[:, :, 0:64], bP3[:, :, 0:64], AF.Copy)
                                    CP(PT[:, :, :], bT[:, 0:256].rearrange("p (c t) -> p c t", t=64))
                            st.append(sl)

                        def s6():
                            ACT(S6[p][:, :, :], PSs_[:, :, 64:128], AF.Copy)
                        st.append(s6)
                        for h2 in range(2):
                            def stok(h2=h2):
                                bank = (B0, B1)[h2]
                                for i2 in range(2):
                                    ii = h2 * 2 + i2
                                    cc = c0 + ii
                                    col = i2 * 192
                                    idf = lambda pa, a: ident_b[pa, a * 64:(a + 1) * 64]
                                    MM2(bank, slice(col, col + 64), lambda pa, a: BKh[pa, cc, 0, :], idf)
                                    MM2(bank, slice(col + 64, col + 128), lambda pa, a: BKh[pa, cc, 1, :], idf)
                                    MM2(bank, slice(col + 128, col + 192), lambda pa, a: vsrc[pa, cc * 64:(cc + 1) * 64], idf)
                                ACT(TOK[p][:, 2 * h2:2 * h2 + 2, :], bank[:, 0:384].rearrange("p (c t) -> p c t", t=192), AF.Copy)
                            st.append(stok)
                        return st

                    def MMg(bank, cols, lhs_fn, rhs_fn, start, stop):
                        for a in range(2):
                            pa = PAS[a]
                            MM(bank[pa, cols], lhs_fn(pa, a), rhs_fn(pa, a), start=start, stop=stop)

                    def rec_stages(gi, chain=0):
                        si, s0, L, c0, first, last = groups[gi]
                        p = gi % 4
                        st = []
                        R0, R1, R2, R3 = (pb[4], pb[5], pb[6], pb[7]) if chain == 0 else (pb[0], pb[1], pb[2], pb[3])
                        Zs, Us, Hbf, Hf = SMALL[chain]
                        c64 = slice(0, 64)
                        if first:
                            def sinit():
                                if ci == 0:
                                    c.dma("sp", out=Hf[:, :], in_=d_state[z, hp])
                                else:
                                    c.op("dve", "memset", Hf[:, :], 0.0)
                                ACT(Hbf[:, :], Hf[:, :], AF.Copy)
                            st.append(sinit)
                        for ii in range(4):
                            cc = c0 + ii

                            def sz(ii=ii, cc=cc):
                                V_f = lambda pa, a: TOK[p][pa, ii, 128:192]
                                MMg(R0, c64, lambda pa, a: MAB4[p][pa, ii, 128:192], V_f, True, False)
                                MMg(R0, c64, lambda pa, a: KR[pa, cc, 0, :], lambda pa, a: Hbf[pa, :], False, True)
                                MMg(R3, c64, V_f, lambda pa, a: MAB4[p][pa, ii, 192:256], True, False)
                                MMg(R3, c64, lambda pa, a: Hbf[pa, :], lambda pa, a: KR[pa, cc, 1, :], False, False)
                                MMg(R2, c64, lambda pa, a: TOK[p][pa, ii, 64:128], V_f, True, False)
                                CP(Zs[:, :], R0[:, 0:64])

                            def su(ii=ii, cc=cc):
                                MMg(R1, c64, lambda pa, a: S6[p][pa, ii, :], lambda pa, a: Zs[pa, :], True, True)
                                TS(Us[:, :], R1[:, 0:64], -1.0, None, ALU.mult)

                            def shy(ii=ii, cc=cc):
                                MMg(R2, c64, lambda pa, a: TOK[p][pa, ii, 0:64], lambda pa, a: Us[pa, :], False, True)
                                MMg(R3, c64, lambda pa, a: Us[pa, :], lambda pa, a: MAB4[p][pa, ii, 64:128], False, True)
                                STT(Hbf[:, :], Hf[:, :], ECb[:, cc:cc + 1], R2[:, 0:64], ALU.mult, ALU.add)
                                STT(Hf[:, :], Hf[:, :], ECb[:, cc:cc + 1], R2[:, 0:64], ALU.mult, ALU.add)
                                ya = yacc_ap(s0, L, cc * 64, cc * 64 + 64)
                                if z == 0:
                                    ACT(ya, R3[:, 0:64], AF.Copy)
                                else:
                                    TT(ya, R3[:, 0:64], ya, ALU.add)
                            st.extend([sz, su, shy])
                        if last and ci == 1:
                            def sfin():
                                c.dma("sp", out=o_news[si, z, hp], in_=Hf[:, :])
                            st.append(sfin)
                        return st

                    ng = len(groups)
                    rounds = [(r, r + 1) for r in range(0, ng, 2)]
                    prev = None
                    for (ga, gb) in rounds:
                        lists = [pre_stages(ga, 0), pre_stages(gb, 1)]
                        if prev is not None:
                            lists.append(rec_stages(prev[0]) + rec_stages(prev[1]))
                        run_interleaved(lists)
                        prev = (ga, gb)
                    if groups[prev[0]][0] != groups[prev[1]][0]:
                        return [rec_stages(prev[0], 0), rec_stages(prev[1], 1)]
                    return [rec_stages(prev[0]) + rec_stages(prev[1])]

                def gn():
                    for (lo, n) in tcs:
                        g0 = tok0 + lo
                        sq, mean, msq, d1, yn = T5[0], T5[1], T5[2], T5[3], T5[4]
                        TT(sq[:, :n], yacc[:, lo:lo + n], yacc[:, lo:lo + n], ALU.mult)
                        ps1 = rotS.next()
                        ps2 = rotS.next()
                        MM(ps1[:, :n], blk_f, yacc[:, lo:lo + n], start=True, stop=True)
                        MM(ps2[:, :n], blk_f, sq[:, :n], start=True, stop=True)
                        ACT(mean[:, :n], ps1[:, :n], AF.Copy, scale=1.0 / 64)
                        TT(msq[:, :n], mean[:, :n], mean[:, :n], ALU.mult)
                        STT(msq[:, :n], ps2[:, :n], 1.0 / 64, msq[:, :n], ALU.mult, ALU.subtract)
                        ACT(msq[:, :n], msq[:, :n], AF.Ln, bias=64e-5, scale=1.0)
                        ACT(msq[:, :n], msq[:, :n], AF.Exp, scale=-0.5)
                        TT(d1[:, :n], yacc[:, lo:lo + n], mean[:, :n], ALU.subtract)
                        TT(d1[:, :n], d1[:, :n], msq[:, :n], ALU.mult)
                        ACT(yn[:, :n], d1[:, :n], AF.Identity, scale=vcol("rln_w", hp), bias=vcol("rln_b", hp))
                        TT(yn[:, :n], yn[:, :n], bonus[:, lo:lo + n], ALU.add)
                        TT(ygT[hp][:, g0:g0 + n], yn[:, :n], sgr[:, lo:lo + n], ALU.mult)
                return base_all, base_steps, scan_z, gn

            P0 = make_pass(*PASSES[0], sgr, bonus)
            P1 = make_pass(*PASSES[1], sgr1, bonus1)
            P0[0]()
            run_interleaved(P0[2](0))
            t_ = P0[2](1)
            if DBG.get('overlap_base', 0):
                bs_ = P1[1]()
                sp_ = DBG.get('base_spacing', 4)
                spaced = []
                for b_ in bs_:
                    spaced.extend([(lambda: None)] * (sp_ - 1) + [b_])
                run_interleaved(t_ + [spaced])
            else:
                run_interleaved(t_)
            P0[3]()
            if not DBG.get('overlap_base', 0):
                P1[0]()
            if hp + 1 < NHP:
                load_weights(hp + 1)
            run_interleaved(P1[2](0))
            run_interleaved(P1[2](1))
            P1[3]()
        handover(False)
        for q_, base_ in enumerate((RSTD[0], E[0])):
            c.op("dve", "memset", PSS[q_][:, 0, 64:65], 0.0)
            c.op("dve", "memset", base_[:, 32:33], 0.0, _xr=[PSSB[q_]])
        for k in range(KC):
            c.dma("sp", out=xT[k][:], in_=xsp[k].t.ap())
        out_proj(d_rwkv_w_out[j])

    def conv_layer(i, j):
        zc = S[0:8]
        zpad = c.view(S[8], "zpadb", [128, 1632], BF16, 0)
        diags = [c.view(wA[0], "diagA", [128, 16, 128], BF16, 0), c.view(wA[1], "diagB", [128, 15, 128], BF16, 0)]
        NU = PTOT - 2 * PADW
        c.op("dve", "memset", zpad[:, 0:PTOT], 0.0)
        wv = d_conv_w_in[j].rearrange("(k p) (n c) -> p k n c", p=128, c=1024)
        pieces_of_tc = {0: [(0, 0, 512)], 1: [(0, 512, 512)], 2: [(1, 0, 256), (2, 0, 256)]}
        nchunks = [(0, 512), (512, 512), (1024, 512), (1536, NU - 1536)]
        for cc in range(KC):
            slot = wI[cc % 2]
            sl4 = slot[:].rearrange("p k (n c) -> p k n c", c=128)
            for n3 in range(3):
                c.dma("pool", out=sl4[:, :, n3, :], in_=wv[:, :, n3, cc * 128:(cc + 1) * 128])
            for k in range(CONVW):
                o = VOFF["cdw_w"] + cc * CONVW + k
                dg = diags[0][:, k, :] if k < 16 else diags[1][:, k - 16, :]
                c.op("dve", "tensor_scalar", out=dg, in0=ident_b, scalar1=vecs[:, o:o + 1], scalar2=None, op0=ALU.mult)
            for tci, tc in enumerate(TCH):
                t0, tl, ci = tc
                psa = rotS.next()
                proj_fm(psa, slot, slice(0, 128), tc)
                psb = rotS.next()
                proj_fm(psb, slot, slice(128, 256), tc)
                sb = rotT.next()
                ACT(sb[:, :tl], psb[:, :tl], AF.Sigmoid)
                for (sg_, l0, n) in pieces_of_tc[tci]:
                    src0 = SEGS[sg_][0] + l0 - t0
                    dst0 = POFF[sg_] + PADW + l0
                    TT(zpad[:, dst0:dst0 + n], psa[:, src0:src0 + n], sb[:, src0:src0 + n], ALU.mult)
            for tci, tc in enumerate(TCH):
                t0, tl, ci = tc
                psg = rotS.next()
                proj_fm(psg, slot, slice(256, 384), tc)
                ACT(ygT[cc][:, t0:t0 + tl], psg[:, :tl], AF.Silu)
            for ni, (u0, n) in enumerate(nchunks):
                psc = pb[4 + ni]
                for k in range(CONVW):
                    dg = diags[0][:, k, :] if k < 16 else diags[1][:, k - 16, :]
                    MM(psc[:, :n], dg, zpad[:, u0 + k:u0 + k + n], start=(k == 0), stop=(k == CONVW - 1))
                ACT(zc[cc][:, u0:u0 + n], psc[:, :n], AF.Identity, bias=vcol("cdw_b", cc))
        for (sg_, l0, n) in [(0, 0, 512), (0, 512, 512), (1, 0, 256), (2, 0, 256)]:
            pos = POFF[sg_] + l0
            tok = SEGS[sg_][0] + l0
            ps1 = rotS.next()
            ps2 = rotS.next()
            for cc in range(KC):
                sq = rotT.next()
                TT(sq[:, :n], zc[cc][:, pos:pos + n], zc[cc][:, pos:pos + n], ALU.mult, eng="pool")
                MM(ps1[:, :n], ones_f, zc[cc][:, pos:pos + n], start=(cc == 0), stop=(cc == KC - 1))
                MM(ps2[:, :n], ones_f, sq[:, :n], start=(cc == 0), stop=(cc == KC - 1))
            mean = rotR.next()
            ACT(mean[:, :n], ps1[:, :n], AF.Copy, scale=1.0 / D)
            msq = rotT.next()
            TT(msq[:, :n], mean[:, :n], mean[:, :n], ALU.mult)
            var = rotT.next()
            STT(var[:, :n], ps2[:, :n], 1.0 / D, msq[:, :n], ALU.mult, ALU.subtract)
            sd = rotT.next()
            ACT(sd[:, :n], var[:, :n], AF.Ln, bias=1e-5, scale=1.0)
            rstd = rotR.next()
            ACT(rstd[:, :n], sd[:, :n], AF.Exp, scale=-0.5)
            for cc in range(KC):
                d1 = rotT.next()
                TT(d1[:, :n], zc[cc][:, pos:pos + n], mean[:, :n], ALU.subtract)
                d2 = rotT.next()
                TT(d2[:, :n], d1[:, :n], rstd[:, :n], ALU.mult)
                s3 = rotT.next()
                ACT(s3[:, :n], d2[:, :n], AF.Silu, scale=vcol("cln_w", cc), bias=vcol("cln_b", cc))
                TT(ygT[cc][:, tok:tok + n], s3[:, :n], ygT[cc][:, tok:tok + n], ALU.mult)
        out_proj(d_conv_w_out[j])

    layer_list = DBG.get("layers", list(range(n_layers)))
    for li, i in enumerate(layer_list):
        kind, j = i % 3, i // 3
        cur["i"] = i
        if li == 0:
            g0_ = adaln_steps(i)
            for _ in range(8):
                next(g0_)
            norm_mod(i)
            for _ in g0_:
                pass
        cur["ada"] = adaln_steps(layer_list[li + 1]) if li + 1 < len(layer_list) else None
        if DBG.get("stop") == "adaln":
            break
        if li > 0:
            norm_mod(i)
        if DBG.get("stop") == "norm":
            break
        if kind == 0:
            attn_layer(i, j)
        elif kind == 1:
            rwkv_layer(i, j)
        else:
            conv_layer(i, j)

    if debug_x:
        for k in range(KC):
            c.dma("sp", out=o_yT[k * 128:(k + 1) * 128, :], in_=xT[k][:])
    else:
        for tc in TCH:
            t0, tl, ci = tc
            rstd = rms_rstd(tc, NORM_EPS, xT)
            for k in range(KC):
                t = rotT.next()
                TT(t[:, :tl], xT[k][:, t0:t0 + tl], rstd[:, :tl], ALU.mult)
                yb = rotT.next()
                ACT(yb[:, :tl], t[:, :tl], AF.Identity, scale=vcol("final_w", k))
                c.dma("sp", out=o_yT[k * 128:(k + 1) * 128, t0:t0 + tl], in_=yb[:, :tl])
    print("SBUF top", c.sb_ptr, "of", SB_END)
    c.emit()
    return nc, c


def rope_tables():
    T = 1024
    t = np.arange(T)
    row = (t // 64).astype(np.float32)
    col = (t % 64).astype(np.float32)
    inv_freq = (10000.0 ** (-np.arange(0, 32, 2, dtype=np.float32) / 32)).astype(np.float32)
    cosT = np.zeros((128, T), np.float32)
    sinT = np.zeros((128, T), np.float32)
    for d in range(128):
        dd = d % 64
        pos = row if dd < 32 else col
        f = dd % 16
        ang = (pos * inv_freq[f]).astype(np.float32)
        cosT[d] = np.cos(ang)
        sinT[d] = np.sin(ang)
    return cosT, sinT


def const_tables():
    ident = np.eye(128, dtype=np.float32)
    ones = np.ones((128, 128), np.float32)
    PR = np.zeros((128, 128), np.float32)
    for m in range(128):
        if m % 32 < 16:
            PR[m + 16, m] = -1.0
        else:
            PR[m - 16, m] = 1.0
    s = np.arange(128)[:, None] % 64
    t = np.arange(64)[None, :]
    maskA = np.concatenate([(s < t), (s <= t)], axis=1).astype(np.float32)
    maskB = np.concatenate([(s > t), (s >= t)], axis=1).astype(np.float32)
    blk = np.zeros((128, 128), np.float32)
    blk[0:64, 0:64] = 1.0
    blk[64:128, 64:128] = 1.0
    ident2 = np.concatenate([np.eye(64, dtype=np.float32), np.eye(64, dtype=np.float32)], axis=0)
    return np.concatenate([ident, ones, PR, maskA, maskB, blk, ident2], axis=1)


def make_in_maps(inp):
    f = lambda a: np.ascontiguousarray(np.asarray(a, np.float32))
    vecs_common = np.zeros((128, NVEC), np.float32)

    def put(name, arr2d):
        o = VOFF[name]
        vecs_common[:, o:o + arr2d.shape[1]] = arr2d

    for i in range(DEPTH):
        put(f"norm_w{i}", fm(inp["norm_w"][i]))
        put(f"ada_b{i}", np.asarray(inp["ada_b"][i], np.float32).reshape(24, 128).T)
    put("final_w", fm(inp["final_norm_w"]))
    put("subln0", np.asarray(inp["attn_subln_w"][0], np.float32).reshape(128, 1))
    put("subln1", np.asarray(inp["attn_subln_w"][1], np.float32).reshape(128, 1))
    for k in range(6):
        put(f"mu{k}", fm(inp["rwkv_mu"][0, k]))
    for z in range(2):
        put(f"w0_{z}", fm(inp["rwkv_w0"][0, z]))
        put(f"a0_{z}", fm(inp["rwkv_a0"][0, z]))
    put("k_k", fm(inp["rwkv_k_k"][0]))
    put("k_a", fm(inp["rwkv_k_a"][0]))
    put("r_k", fm(np.asarray(inp["rwkv_r_k"][0]).reshape(-1)))
    put("rln_w", fm(inp["rwkv_ln_w"][0]))
    put("rln_b", fm(inp["rwkv_ln_b"][0]))
    put("cdw_b", fm(inp["conv_dw_b"][0]))
    put("cln_w", fm(inp["conv_ln_w"][0]))
    put("cln_b", fm(inp["conv_ln_b"][0]))
    dw = np.asarray(inp["conv_dw_w"][0], np.float32)
    dwl = dw.reshape(CONVW, 8, 128).transpose(2, 1, 0).reshape(128, 8 * CONVW)
    put("cdw_w", dwl)

    lam_bc = np.ascontiguousarray(np.broadcast_to(np.asarray(inp["attn_lambda"], np.float32).reshape(1, 512), (128, 512)))
    consts = const_tables()
    cosT, sinT = rope_tables()
    w1cat = f(np.concatenate([inp["rwkv_w1"][0, 0], inp["rwkv_w1"][0, 1]], axis=1))
    a1cat = f(np.concatenate([inp["rwkv_a1"][0, 0], inp["rwkv_a1"][0, 1]], axis=1))
    w2cat = f(np.concatenate([inp["rwkv_w2"][0, 0], inp["rwkv_w2"][0, 1]], axis=0))
    a2cat = f(np.concatenate([inp["rwkv_a2"][0, 0], inp["rwkv_a2"][0, 1]], axis=0))
    shared = {
        "vecs": vecs_common, "lam_bc": lam_bc, "consts": consts, "cosT": cosT, "sinT": sinT,
        "ada_w": f(inp["ada_w"]), "attn_w_in": f(inp["attn_w_in"]), "attn_w_out": f(inp["attn_w_out"]),
        "conv_w_in": f(inp["conv_w_in"]), "conv_w_out": f(inp["conv_w_out"]),
        "rwkv_w_in": f(inp["rwkv_w_in"]), "rwkv_w_out": f(inp["rwkv_w_out"]),
        "rw_w1cat": w1cat, "rw_a1cat": a1cat, "rw_w2cat": w2cat, "rw_a2cat": a2cat,
    }
    maps = []
    xs, xp = np.asarray(inp["x_sample"], np.float32), np.asarray(inp["x_prompt"], np.float32)
    for b in range(8):
        m = dict(shared)
        m["xT_in"] = np.ascontiguousarray(np.concatenate([xs[b].T, xp[2 * b].T, xp[2 * b + 1].T], axis=1))
        cT = np.stack([fm(inp["c"][b]), fm(inp["c_ctx"])], axis=2).reshape(128, 16)
        m["condT"] = np.ascontiguousarray(cT)
        m["ckT"] = np.ascontiguousarray(np.asarray(inp["cache_attn_k"][b], np.float32).transpose(0, 1, 3, 2))
        m["cv"] = f(inp["cache_attn_v"][b])
        st = np.asarray(inp["state_rwkv"][b, 0], np.float32)
        m["rw_state"] = np.ascontiguousarray(st.transpose(0, 1, 3, 2).reshape(2, 8, 128, 64))
        maps.append(m)
    return maps


def assemble(results):
    y_prompt = np.zeros((16, 256, 1024), np.float32)
    y_sample = np.zeros((8, 1024, 1024), np.float32)
    new_k = np.zeros((16, 2, 8, 256, 128), np.float32)
    new_v = np.zeros((16, 2, 8, 256, 128), np.float32)
    new_s = np.zeros((16, 1, 2, 16, 64, 64), np.float32)
    for b in range(8):
        r = results[b]
        yT = r["yT"]
        y_sample[b] = yT[:, 0:1024].T
        y_prompt[2 * b] = yT[:, 1024:1280].T
        y_prompt[2 * b + 1] = yT[:, 1280:1536].T
        nk = r["newk"]
        nv = r["newv"]
        for p in range(2):
            new_k[2 * b + p] = nk[:, :, :, 256 * p:256 * (p + 1)].transpose(0, 1, 3, 2)
            new_v[2 * b + p] = nv[:, :, 256 * p:256 * (p + 1), :]
            ns = r["news"][p]
            new_s[2 * b + p, 0] = ns.reshape(2, 16, 64, 64).transpose(0, 1, 3, 2)
    return (y_prompt, y_sample, new_k, new_v, new_s)


_CACHE = {}


def kernel(**inputs):
    if "prog" not in _CACHE:
        _CACHE["prog"] = build_program()
    nc, c = _CACHE["prog"]
    in_maps = make_in_maps(inputs)
    res = run_bass_kernel_spmd(nc, in_maps, core_ids=list(range(8)))
    return assemble(res.results)
```
